# Optimizing a Trainium2 kernel written in Bass

```python
import jax, jax.numpy as jnp
from jax import lax
import numpy as np

D_MODEL = 1024
BATCH = 8
SEQ = 4096
DEPTH = 4

MEM_LEN = 256
HEAD_DIM = 64
EPS = 1e-6
GM_WIDTH = D_MODEL // 4
GM_GROUPS = GM_WIDTH // HEAD_DIM
CHUNK = 128
POOL_WIDTH = D_MODEL // 4
POOL_WINDOWS = (2, 4, 8, 16)
POOL_GROUPS = len(POOL_WINDOWS)
POOL_GROUP_DIM = POOL_WIDTH // POOL_GROUPS
ATT_WIDTH = D_MODEL // 2
ATT_Q_HEADS = ATT_WIDTH // HEAD_DIM
ATT_KV_HEADS = ATT_Q_HEADS // 4
ATT_GROUP = ATT_Q_HEADS // ATT_KV_HEADS
WINDOW = 128
ROPE_THETA = 10000.0
D_MIX = GM_WIDTH + POOL_WIDTH + ATT_WIDTH
IN_SIZES = (2 * GM_WIDTH, POOL_WIDTH, ATT_Q_HEADS * HEAD_DIM, ATT_KV_HEADS * HEAD_DIM, ATT_KV_HEADS * HEAD_DIM)
IN_SPLITS = tuple(int(s) for s in np.cumsum(IN_SIZES)[:-1])
D_IN = sum(IN_SIZES)
X_HEADS = 4
X_HEAD_DIM = D_MODEL // X_HEADS
D_FF = -(-8 * D_MODEL // (3 * 256)) * 256

kernel_name = "hybrid_gmlp_pool_swa_sink_trunk"


def rms_norm(x, g):
    xf = x.astype(jnp.float32)
    y = xf * lax.rsqrt(jnp.mean(xf * xf, axis=-1, keepdims=True) + EPS)
    return (y * g.astype(jnp.float32)).astype(x.dtype)


def spatial_gating(uv, v_gain, w_s, b_s):
    B, S, _ = uv.shape
    u, v = jnp.split(uv, 2, axis=-1)
    v = v.reshape(B, S // CHUNK, CHUNK, GM_GROUPS, HEAD_DIM)
    v = rms_norm(v, v_gain.reshape(GM_GROUPS, HEAD_DIM))
    causal = jnp.tril(jnp.ones((CHUNK, CHUNK), dtype=bool))
    w = jnp.where(causal[None], w_s, jnp.zeros_like(w_s))
    mixed = jnp.einsum('gts,bcsgd->bctgd', w, v) + b_s.T[None, None, :, :, None]
    return u * mixed.reshape(B, S, GM_WIDTH)


def multiscale_pool(p, pool_w, pool_scale):
    B, S, _ = p.shape
    pf = p.astype(jnp.float32)
    cs = jnp.pad(jnp.cumsum(pf, axis=1), ((0, 0), (1, 0), (0, 0)))
    t = jnp.arange(S)
    outs = []
    for g, w in enumerate(POOL_WINDOWS):
        sl = slice(g * POOL_GROUP_DIM, (g + 1) * POOL_GROUP_DIM)
        hi = cs[:, 1:, sl]
        lo = jnp.pad(cs[:, :S - w + 1, sl], ((0, 0), (w - 1, 0), (0, 0)))
        count = jnp.minimum(t + 1, w).astype(jnp.float32)[None, :, None]
        outs.append((hi - lo) / count - pf[:, :, sl])
    pooled = jnp.stack(outs, axis=2).astype(p.dtype)
    mapped = jnp.einsum('bsgc,gcd->bsgd', pooled, pool_w).reshape(B, S, POOL_WIDTH)
    return mapped * pool_scale


def rope(x, positions):
    half = HEAD_DIM // 2
    inv = ROPE_THETA ** (-jnp.arange(half, dtype=jnp.float32) / half)
    ang = positions.astype(jnp.float32)[..., None] * inv
    cos = jnp.cos(ang)[:, :, None, :]
    sin = jnp.sin(ang)[:, :, None, :]
    xf = x.astype(jnp.float32)
    x1, x2 = xf[..., :half], xf[..., half:]
    return jnp.concatenate([x1 * cos - x2 * sin, x2 * cos + x1 * sin], axis=-1).astype(x.dtype)


def sliding_window_attention(q, k, v, sinks):
    B, S, _, _ = q.shape
    NB = S // WINDOW
    qb = q.reshape(B, NB, WINDOW, ATT_KV_HEADS, ATT_GROUP, HEAD_DIM)

    def band(t_):
        tb = t_.reshape(B, NB, WINDOW, ATT_KV_HEADS, HEAD_DIM)
        prev = jnp.pad(tb[:, :-1], ((0, 0), (1, 0), (0, 0), (0, 0), (0, 0)))
        return jnp.concatenate([prev, tb], axis=2)

    kb, vb = band(k), band(v)
    scores = jnp.einsum('bnqhgd,bnkhd->bnhgqk', qb, kb,
                        preferred_element_type=jnp.float32) * (HEAD_DIM ** -0.5)
    qi = jnp.arange(WINDOW)[:, None] + WINDOW
    ki = jnp.arange(2 * WINDOW)[None, :]
    rel = qi - ki
    valid = (rel >= 0) & (rel < WINDOW)
    valid = valid[None] & ((jnp.arange(NB) > 0)[:, None, None] | (ki >= WINDOW)[None])
    scores = jnp.where(valid[None, :, None, None], scores, -jnp.inf)
    sink = jnp.broadcast_to(
        sinks.astype(jnp.float32).reshape(ATT_KV_HEADS, ATT_GROUP)[None, None, :, :, None, None],
        scores.shape[:-1] + (1,))
    probs = jax.nn.softmax(jnp.concatenate([scores, sink], axis=-1), axis=-1)[..., :-1]
    out = jnp.einsum('bnhgqk,bnkhd->bnqhgd', probs.astype(v.dtype), vb)
    return out.reshape(B, S, ATT_WIDTH)


def cross_attention(h, mem_n, w_xq, w_xkv, w_xo):
    B, S, _ = h.shape
    M = mem_n.shape[1]
    q = (h @ w_xq).reshape(B, S, X_HEADS, X_HEAD_DIM)
    k, v = jnp.split(mem_n @ w_xkv, 2, axis=-1)
    k = k.reshape(B, M, X_HEADS, X_HEAD_DIM)
    v = v.reshape(B, M, X_HEADS, X_HEAD_DIM)
    s = jnp.einsum('bshd,bmhd->bhsm', q, k, preferred_element_type=jnp.float32) * (X_HEAD_DIM ** -0.5)
    p = jax.nn.softmax(s, axis=-1).astype(v.dtype)
    o = jnp.einsum('bhsm,bmhd->bshd', p, v).reshape(B, S, D_MODEL)
    return o @ w_xo


def setup_inputs(seed: int = 0) -> dict:
    key = jax.random.key(seed)
    ks = jax.random.split(key, 32)
    f32 = jnp.float32

    def nrm(k, shape, scale):
        return jax.random.normal(k, shape, f32) * scale

    def gain(k, shape):
        return 1.0 + 0.05 * jax.random.normal(k, shape, f32)

    offsets = jax.random.randint(ks[2], (BATCH, 1), 0, 1024, dtype=jnp.int32)
    positions = offsets + jnp.arange(SEQ, dtype=jnp.int32)[None, :]
    return {
        "x": nrm(ks[0], (BATCH, SEQ, D_MODEL), 1.0),
        "mem": nrm(ks[1], (BATCH, MEM_LEN, D_MODEL), 1.0),
        "positions": positions,
        "mem_norm_g": gain(ks[3], (D_MODEL,)),
        "mix_pre_g": gain(ks[4], (DEPTH, D_MODEL)),
        "mix_post_g": gain(ks[5], (DEPTH, D_MODEL)),
        "w_in": nrm(ks[6], (DEPTH, D_MODEL, D_IN), D_MODEL ** -0.5),
        "gm_v_g": gain(ks[7], (DEPTH, GM_WIDTH)),
        "gm_w_s": nrm(ks[8], (DEPTH, GM_GROUPS, CHUNK, CHUNK), 0.5 * CHUNK ** -0.5),
        "gm_b_s": 1.0 + 0.1 * jax.random.normal(ks[9], (DEPTH, GM_GROUPS, CHUNK), f32),
        "pool_w": nrm(ks[10], (DEPTH, POOL_GROUPS, POOL_GROUP_DIM, POOL_GROUP_DIM), POOL_GROUP_DIM ** -0.5),
        "pool_scale": 1.0 + 0.1 * jax.random.normal(ks[11], (DEPTH, POOL_WIDTH), f32),
        "attn_sinks": nrm(ks[12], (DEPTH, ATT_Q_HEADS), 1.0),
        "w_o": nrm(ks[13], (DEPTH, D_MIX, D_MODEL), D_MIX ** -0.5),
        "x_pre_g": gain(ks[14], (DEPTH, D_MODEL)),
        "x_post_g": gain(ks[15], (DEPTH, D_MODEL)),
        "w_xq": nrm(ks[16], (DEPTH, D_MODEL, D_MODEL), D_MODEL ** -0.5),
        "w_xkv": nrm(ks[17], (DEPTH, D_MODEL, 2 * D_MODEL), D_MODEL ** -0.5),
        "w_xo": nrm(ks[18], (DEPTH, D_MODEL, D_MODEL), D_MODEL ** -0.5),
        "ffn_pre_g": gain(ks[19], (DEPTH, D_MODEL)),
        "ffn_post_g": gain(ks[20], (DEPTH, D_MODEL)),
        "w_gate_up": nrm(ks[21], (DEPTH, D_MODEL, 2 * D_FF), D_MODEL ** -0.5),
        "w_down": nrm(ks[22], (DEPTH, D_FF, D_MODEL), D_FF ** -0.5),
    }


def reference(x, mem, positions, mem_norm_g, mix_pre_g, mix_post_g, w_in, gm_v_g, gm_w_s, gm_b_s,
              pool_w, pool_scale, attn_sinks, w_o, x_pre_g, x_post_g, w_xq, w_xkv, w_xo,
              ffn_pre_g, ffn_post_g, w_gate_up, w_down):
    B, S, _ = x.shape
    mem_n = rms_norm(mem, mem_norm_g)
    for l in range(DEPTH):
        h = rms_norm(x, mix_pre_g[l])
        z = h @ w_in[l]
        z_gm, z_pool, z_q, z_k, z_v = jnp.split(z, IN_SPLITS, axis=-1)
        a = spatial_gating(jax.nn.gelu(z_gm), gm_v_g[l], gm_w_s[l], gm_b_s[l])
        b = multiscale_pool(z_pool, pool_w[l], pool_scale[l])
        q = rope(z_q.reshape(B, S, ATT_Q_HEADS, HEAD_DIM), positions)
        k = rope(z_k.reshape(B, S, ATT_KV_HEADS, HEAD_DIM), positions)
        v = z_v.reshape(B, S, ATT_KV_HEADS, HEAD_DIM)
        c = sliding_window_attention(q, k, v, attn_sinks[l])
        mix = jnp.concatenate([a, b, c], axis=-1) @ w_o[l]
        x = x + rms_norm(mix, mix_post_g[l])
        h = rms_norm(x, x_pre_g[l])
        x = x + rms_norm(cross_attention(h, mem_n, w_xq[l], w_xkv[l], w_xo[l]), x_post_g[l])
        h = rms_norm(x, ffn_pre_g[l])
        gate, up = jnp.split(h @ w_gate_up[l], 2, axis=-1)
        f = (jax.nn.silu(gate) * up) @ w_down[l]
        x = x + rms_norm(f, ffn_post_g[l])
    return x
```

```python
import numpy as np
from contextlib import ExitStack
import concourse.bass as bass
import concourse.mybir as mybir
from concourse.bass_utils import run_bass_kernel_spmd

F32 = mybir.dt.float32
BF16 = mybir.dt.bfloat16
I32 = mybir.dt.int32
AF = mybir.ActivationFunctionType
ALU = mybir.AluOpType
AX = mybir.AxisListType

D = 1024
SEQ = 4096
DEPTH = 4
MEM = 256
DFF = 2816
T = 512
NB = 4
EPS = 1e-6
PRE_RS_BF16 = False
RSTD_LN_EXP = True
NWA = 92
OFF_WIN, OFF_WO, OFF_XQ, OFF_XO, OFF_GU, OFF_XK = 0, 16, 24, 32, 40, 84
NPA = 192 + 8 + 8 + 32 + 1024
NPB = 2048 + 1024 + 1024
NCONST = 128 * 4 + 1 + 1 + 32 + 1
TWO_PI = float(2 * np.pi)
C1 = 6.28125
C2 = float(2 * np.pi - 6.28125)


class Buf:
    __slots__ = ("name", "last_w", "readers", "excl")

    def __init__(self, name, excl=False):
        self.name = name
        self.last_w = None
        self.readers = []
        self.excl = excl


class Op:
    __slots__ = ("eng", "fn", "deps", "sig", "sem", "val", "is_dma", "waits", "clock", "idx", "ndma")


class Sched:
    ENGS = ("pe", "act", "dve", "pool", "sp")

    def __init__(self, nc, stack):
        self.nc = nc
        self.stack = stack
        self.ops = {e: [] for e in self.ENGS}
        self.all = []
        self.eng_sem = {}
        for e in ("pe", "act", "dve", "pool"):
            self.eng_sem[e] = stack.enter_context(nc.semaphore("s_" + e))
        self.slot_sem = {}
        self.slot_cnt = {}

    def _slot(self, slot):
        if slot not in self.slot_sem:
            self.slot_sem[slot] = self.stack.enter_context(self.nc.semaphore("d_" + slot))
            self.slot_cnt[slot] = 0
        return self.slot_sem[slot]

    def op(self, eng, fn, reads=(), writes=(), dma_slot=None, ndma=1):
        o = Op()
        o.eng = eng
        o.fn = fn
        o.is_dma = dma_slot is not None
        o.ndma = ndma
        o.sig = False
        o.waits = []
        o.idx = len(self.all)
        ex = [b for b in reads if b.excl]
        if ex:
            reads = [b for b in reads if not b.excl]
            writes = list(writes) + [b for b in ex if b not in writes]
        deps = []
        for b in reads:
            if b.last_w is not None:
                deps.append((b.last_w, 0))
        for b in writes:
            if b.last_w is not None:
                deps.append((b.last_w, 1))
            for r in b.readers:
                deps.append((r, 1))
        o.deps = []
        seen = set()
        for p, kind in deps:
            if p is o or id(p) in seen:
                continue
            if (not o.is_dma) and (not p.is_dma) and p.eng == eng and kind != 0 and eng == "pe":
                continue
            seen.add(id(p))
            o.deps.append(p)
            p.sig = True
        if o.is_dma:
            o.sem = self._slot(dma_slot)
            self.slot_cnt[dma_slot] += 16 * ndma
            o.val = self.slot_cnt[dma_slot]
            o.sig = True
        else:
            o.sem = self.eng_sem.get(eng)
            o.val = None
        for b in writes:
            b.last_w = o
            b.readers = []
        for b in reads:
            if b.last_w is not o:
                b.readers.append(o)
        self.ops[eng].append(o)
        self.all.append(o)
        return o

    def finalize(self):
        cnt = {e: 0 for e in self.eng_sem}
        for o in self.all:
            if not o.is_dma and o.sig:
                cnt[o.eng] += 1
                o.val = cnt[o.eng]
        clock = {e: {} for e in self.ENGS}
        for o in self.all:
            ck = clock[o.eng]
            for p in sorted(o.deps, key=lambda p: -p.idx):
                key = id(p.sem)
                if ck.get(key, 0) >= p.val:
                    continue
                o.waits.append((p.sem, p.val))
                for k, v in p.clock.items():
                    if ck.get(k, 0) < v:
                        ck[k] = v
                if ck.get(key, 0) < p.val:
                    ck[key] = p.val
            o.clock = dict(ck)
            if o.sig:
                o.clock[id(o.sem)] = max(o.clock.get(id(o.sem), 0), o.val)

    def emit(self):
        self.finalize()
        nc = self.nc

        def run(e, ops):
            for o in ops:
                for sem, val in o.waits:
                    e.wait_ge(sem, val)
                ins = o.fn(e)
                if o.is_dma:
                    lst = ins if isinstance(ins, (list, tuple)) else [ins]
                    assert len(lst) == o.ndma
                    for i_ in lst:
                        i_.then_inc(o.sem, 16)
                elif o.sig:
                    ins.then_inc(o.sem, 1)

        with nc.Block() as block:
            @block.tensor
            def _(e):
                run(e, self.ops["pe"])

            @block.scalar
            def _(e):
                run(e, self.ops["act"])

            @block.vector
            def _(e):
                run(e, self.ops["dve"])

            @block.gpsimd
            def _(e):
                run(e, self.ops["pool"])

            @block.sync
            def _(e):
                run(e, self.ops["sp"])


class Rot:
    def __init__(self, items):
        self.items = items
        self.i = 0

    def next(self):
        it = self.items[self.i % len(self.items)]
        self.i += 1
        return it


class _Stop(Exception):
    pass


def build_program(n_tiles=SEQ // T, depth=DEPTH, stop=None):
    nc = bass.Bass("TRN2", target_bir_lowering=False)
    seq = n_tiles * T
    L = depth

    def din(name, shape, dt=F32):
        return nc.dram_tensor(name, list(shape), dt, kind="ExternalInput").ap()

    x_d = din("x", [seq, D])
    mem_d = din("mem", [MEM, D])
    pos_d = din("pos", [1, seq], I32)
    cst_d = din("cst", [128, NCONST])
    pa_d = din("pa", [128, NPA])
    pb_d = din("pb", [128, NPB])
    wa_d = din("wa", [L, NWA, 128, 1024])
    wd_d = din("wd", [L, 8, 128, 2816])
    wt_d = din("wt", [L, 128, 8 * 384])
    wv_d = din("wv", [L, 2, 128, 4096])
    out_d = nc.dram_tensor("out", [seq, D], F32, kind="ExternalOutput").ap()
    was = nc.dram_tensor("was", [L, NWA, 128, 1024], BF16, kind="Internal").ap()
    wds = nc.dram_tensor("wds", [L, 8, 128, 2816], BF16, kind="Internal").ap()
    wts = nc.dram_tensor("wts", [L, 128, 8 * 384], BF16, kind="Internal").ap()
    wvs = nc.dram_tensor("wvs", [L, 2, 128, 4096], BF16, kind="Internal").ap()

    st = ExitStack()
    with st:
        S = Sched(nc, st)

        def sb(name, shape, dt):
            return st.enter_context(nc.sbuf_tensor("sb_" + name, list(shape), dt))

        def psum(name, shape, dt):
            return st.enter_context(nc.psum_tensor("pp_" + name, list(shape), dt))

        xT = sb("xT", [128, 8, T], F32)
        xT_b = [Buf("xT%d" % k) for k in range(8)]
        stage = sb("stage", [128, 4096], F32)
        st_b = [Buf("st%d" % k) for k in range(8)]
        hT = sb("hT", [128, 8, T], BF16)
        hT_b = [Buf("hT%d" % k) for k in range(8)]
        sqb_t = sb("sqb", [128, 4, T], BF16)
        sqb = Rot([(sqb_t[:, i, :], Buf("sqb%d" % i)) for i in range(4)])
        NF = 4
        ft_t = sb("ftmp", [128, NF, T], F32)
        ftmp = Rot([(ft_t[:, i, :], Buf("ft%d" % i)) for i in range(NF)])
        bp_t = sb("bpool", [128, 24, T], BF16)
        bp_b = [Buf("bp%d" % k) for k in range(24)]
        uT = sb("uT", [128, 2, T], F32)
        uT_b = [Buf("uT0"), Buf("uT1")]
        pT = sb("pT", [128, 2, T + 16], F32)
        pT_b = Buf("pT")
        pooled = sb("pooled", [128, 2, T], BF16)
        pooled_b = Buf("pooled")
        QT = sb("QT", [128, 4, T], BF16)
        QT_b = [Buf("QT%d" % k) for k in range(4)]
        KT2 = sb("KT2", [128, 2, 2, 128 + T], BF16)
        KT2_b = Buf("KT2")
        Kprev = sb("Kprev", [128, L, 4, 128], BF16)
        Kprev_b = [Buf("Kprev%d" % l) for l in range(L)]
        Vaug = sb("Vaug", [128, 5, 2, 80], BF16)
        Vaug_b = Buf("Vaug")
        Vprev = sb("Vprev", [128, L, 2, 80], BF16)
        Vprev_b = [Buf("Vprev%d" % l) for l in range(L)]
        phalo = sb("phalo", [128, L, 2, 16], F32)
        phalo_b = [Buf("phalo%d" % l) for l in range(L)]
        cosT = sb("cosT", [128, T], F32)
        sinT = sb("sinT", [128, T], F32)
        cs_b = Buf("cossin")
        vn_pad = sb("vn_pad", [128, 4, 4, 128], BF16)
        vn_b = [Buf("vn%d" % k) for k in range(4)]
        pt_t = sb("PT", [128, 4, T], BF16)
        ptr = Rot([(pt_t[:, i, :], Buf("PT%d" % i)) for i in range(4)])
        ct_t = sb("ctok", [128, 2, T], BF16)
        ctr = Rot([(ct_t[:, i, :], Buf("ctok%d" % i)) for i in range(2)])
        px_t = sb("PxT", [128, 4, T], BF16)
        pxr = Rot([(px_t[:, i, :], Buf("PxT%d" % i)) for i in range(4)])
        memT = sb("memT", [128, 8, MEM], BF16)
        memT_b = Buf("memT")
        KmT = sb("KmT", [128, L, 8, MEM], BF16)
        KmT_b = [Buf("KmT%d" % l) for l in range(L)]
        Vm = sb("Vm", [128, L, 2, D], BF16)
        Vm_b = [Buf("Vm%d" % l) for l in range(L)]
        pa = sb("pa", [128, NPA], F32)
        pa_b = Buf("pa")
        esink = sb("esink", [128, 32], F32)
        esink_b = Buf("esink")
        WsT = sb("WsT", [128, L, 4, 128], BF16)
        BD = sb("BD", [128, L, 2, 128], BF16)
        wsbd_b = Buf("wsbd")
        cst = sb("cst", [128, NCONST], F32)
        cst_b = Buf("cst")
        identB = sb("identB", [128, 128], BF16)
        mbias = sb("mbias", [128, 2, 512], BF16)
        onesN = sb("onesN", [128, 128], BF16)
        ones1 = sb("ones1", [128, 128], BF16)
        cb_b = Buf("constsB")
        small = sb("small", [128, 96], F32)
        posi = sb("posi", [128, T], I32)
        posi_b = Buf("posi")
        NSLOT = 4
        ring_t = sb("ring", [128, NSLOT, 4096], BF16)
        ring = Rot([(ring_t[:, i, :], Buf("ring%d" % i)) for i in range(NSLOT)])
        ringcnt = [0]

        ps_t = [psum("ps%d" % i, [128, 512], F32) for i in range(8)]
        psr = Rot([(ps_t[i], Buf("ps%d" % i, True)) for i in range(7)])
        ss_ps, ss_b = ps_t[7], Buf("ss", True)

        identF = cst[:, 0:128]
        trilT = cst[:, 128:256]
        mc_f = cst[:, 256:384]
        mp_f = cst[:, 384:512]
        inv_col = cst[:, 512:513]
        sgn_col = cst[:, 513:514]
        invcnt = cst[:, 514:546]
        eps_col = cst[:, 546:547]

        def G(l, j):
            o = (l * 6 + j) * 8
            return pa[:, o:o + 8]

        def gmg(l):
            return pa[:, 192 + 2 * l:192 + 2 * l + 2]

        def psc(l):
            return pa[:, 200 + 2 * l:200 + 2 * l + 2]

        def bfull(l, a):
            o = 240 + (l * 2 + a) * 128
            return pa[:, o:o + 128]

        act_or_dve = Rot(["act", "dve"])

        def multi(eng, fns, reads, writes):
            for f_ in fns:
                S.op(eng, f_, reads, writes)

        def copy_op(eng, out, in_, reads, writes):
            if eng == "act":
                S.op("act", lambda e: e.activation(out=out, in_=in_, func=AF.Copy), reads, writes)
            else:
                S.op(eng, lambda e: e.tensor_copy(out=out, in_=in_), reads, writes)

        split_next = [0]

        def mm_group(out, pairs, reads, writes):
            if split_next[0] > 0 and len(pairs) == 8 and hT_b[0] in reads:
                split_next[0] -= 1
                other = [b_ for b_ in reads if b_ not in hT_b]
                for i, (l_, r_) in enumerate(pairs):
                    S.op("pe", lambda e, l_=l_, r_=r_, i=i: e.matmul(out, lhsT=l_, rhs=r_, start=(i == 0), stop=(i == 7)),
                         other + [hT_b[i]], writes)
                return

            def fn(e):
                n = len(pairs)
                last = None
                for i, (l_, r_) in enumerate(pairs):
                    last = e.matmul(out, lhsT=l_, rhs=r_, start=(i == 0), stop=(i == n - 1))
                return last
            S.op("pe", fn, reads, writes)

        def ring_load(src_ap, nelem, slot_name="ring"):
            ap, b = ring.next()
            k = ringcnt[0] % NSLOT
            ringcnt[0] += 1
            S.op("sp", lambda e: e.dma_start(out=ap[:, 0:nelem], in_=src_ap), reads=[scr_b[src_ap.tensor.name]],
                 writes=[b], dma_slot="ring%d" % k)
            return ap, b

        scr_b = {"was": Buf("was"), "wds": Buf("wds"), "wts": Buf("wts"), "wvs": Buf("wvs")}
        S.op("sp", lambda e: e.dma_start(out=cst[:], in_=cst_d[:, :]), writes=[cst_b], dma_slot="cst")
        S.op("sp", lambda e: e.dma_start(out=pa[:], in_=pa_d[:, :]), writes=[pa_b], dma_slot="pa")
        S.op("sp", lambda e: e.dma_start(out=stage[:, 0:NPB], in_=pb_d[:, :]), writes=st_b, dma_slot="pb")

        conv_ops = {}

        def conv(l, name, lst):
            def fn(e):
                return [e.dma_start(out=o_, in_=i_) for (o_, i_) in lst]
            conv_ops[(l, name)] = S.op("pool", fn, writes=[], dma_slot="cv_%d_%s" % (l, name), ndma=len(lst))

        cv_b = {}

        def emit_conv(l, name):
            def pairs(c0s):
                return [(was[l, c0:c0 + 4].rearrange("c p f -> p c f"), wa_d[l, c0:c0 + 4].rearrange("c p f -> p c f")) for c0 in c0s]
            if name == "win":
                lst = pairs(range(0, 16, 4)) + [(wts[l], wt_d[l])]
            elif name == "mid":
                lst = pairs(list(range(16, 40, 4)) + [84, 88]) + [(wvs[l, h], wv_d[l, h]) for h in range(2)]
            elif name == "gu":
                lst = pairs(range(40, 84, 4))
            else:
                lst = [(wds[l, c], wd_d[l, c]) for c in range(8)]
            conv(l, name, lst)
            b = Buf("cv%d%s" % (l, name))
            b.last_w = conv_ops[(l, name)]
            cv_b[(l, name)] = b
        emit_conv(0, "win")

        def wa_group(c):
            if c < 16:
                return "win"
            if c < 40 or c >= 84:
                return "mid"
            return "gu"

        def load_wa(l, c0, n=4):
            ap, b = ring.next()
            k = ringcnt[0] % NSLOT
            ringcnt[0] += 1
            src = was[l, c0:c0 + n].rearrange("c p f -> p c f")
            dst = ap[:, 0:n * 1024].rearrange("p (c f) -> p c f", c=n)
            S.op("sp", lambda e: e.dma_start(out=dst, in_=src), reads=[cv_b[(l, wa_group(c0))]], writes=[b],
                 dma_slot="ring%d" % k)
            return ap.rearrange("p (c k j) -> p c k j", c=4, k=8), b

        def load_flat(src, nelem, cvkey):
            ap, b = ring.next()
            k = ringcnt[0] % NSLOT
            ringcnt[0] += 1
            S.op("sp", lambda e: e.dma_start(out=ap[:, 0:nelem], in_=src), reads=[cv_b[cvkey]], writes=[b],
                 dma_slot="ring%d" % k)
            return ap, b

        multi("dve", [lambda e: e.memset(onesN[:], 1.0 / 1024.0),
                      lambda e: e.memset(ones1[:], 1.0),
                      lambda e: e.tensor_copy(out=identB[:], in_=identF),
                      lambda e: e.tensor_scalar(out=mbias[:, 0, :].rearrange("p (g q) -> p g q", g=4),
                                                in0=mc_f.rearrange("p (o q) -> p o q", o=1).broadcast_to([128, 4, 128]),
                                                scalar1=-1.0, scalar2=30000.0, op0=ALU.add, op1=ALU.mult),
                      lambda e: e.tensor_scalar(out=mbias[:, 1, :].rearrange("p (g q) -> p g q", g=4),
                                                in0=mp_f.rearrange("p (o q) -> p o q", o=1).broadcast_to([128, 4, 128]),
                                                scalar1=-1.0, scalar2=30000.0, op0=ALU.add, op1=ALU.mult)],
              reads=[cst_b], writes=[cb_b])

        S.op("dve", lambda e: e.memset(Vaug[:], 1.0), writes=[Vaug_b])
        S.op("dve", lambda e: e.memset(Vprev[:], 1.0), writes=Vprev_b)
        S.op("dve", lambda e: e.memset(Kprev[:], 0.0), writes=Kprev_b)
        S.op("dve", lambda e: e.memset(phalo[:], 0.0), writes=phalo_b)
        S.op("dve", lambda e: e.memset(KT2[:], 0.0), writes=[KT2_b])
        S.op("dve", lambda e: e.memset(pT[:], 0.0), writes=[pT_b])
        S.op("dve", lambda e: e.memset(vn_pad[:], 0.0), writes=vn_b)
        S.op("act", lambda e: e.activation(out=esink[:], in_=pa[:, 208:240], func=AF.Exp), reads=[pa_b], writes=[esink_b])
        multi("dve", [lambda e: e.tensor_tensor(out=WsT[:].rearrange("p l g t -> p (l g) t"),
                                                in0=stage[:, 0:4 * L * 128].rearrange("p (m t) -> p m t", t=128),
                                                in1=trilT.rearrange("p (o t) -> p o t", o=1).broadcast_to([128, 4 * L, 128]),
                                                op=ALU.mult),
                      lambda e: e.tensor_copy(out=BD[:].rearrange("p l a d -> p (l a d)"), in_=stage[:, 2048:3072][:, 0:L * 256])],
              reads=st_b + [cst_b], writes=[wsbd_b])

        mem_sb = xT[:].rearrange("p k t -> p (k t)")
        S.op("sp", lambda e: e.dma_start(out=mem_sb[:, 0:2048].rearrange("p (c d) -> p c d", c=2),
                                         in_=mem_d.rearrange("(c p) d -> p c d", p=128)), writes=xT_b, dma_slot="mem")
        memg = stage[:, 3072:4096]

        sm_b = Buf("small_mem")

        multi("act", [lambda e, c=c: e.activation(out=mem_sb[:, 2048 + c * 1024:2048 + (c + 1) * 1024],
                                                  in_=mem_sb[:, c * 1024:(c + 1) * 1024],
                                                  func=AF.Square, accum_out=small[:, c:c + 1]) for c in range(2)],
              reads=xT_b, writes=xT_b + [sm_b])
        S.op("act", lambda e: e.activation(out=small[:, 2:4], in_=small[:, 0:2], func=AF.Sqrt, bias=eps_col, scale=1.0 / D),
             reads=[sm_b, cst_b], writes=[sm_b])
        S.op("dve", lambda e: e.reciprocal(out=small[:, 4:6], in_=small[:, 2:4]), reads=[sm_b], writes=[sm_b])

        multi("dve", [lambda e, c=c: e.scalar_tensor_tensor(out=mem_sb[:, c * 1024:(c + 1) * 1024],
                                                            in0=mem_sb[:, c * 1024:(c + 1) * 1024],
                                                            scalar=small[:, 4 + c:5 + c], in1=memg, op0=ALU.mult, op1=ALU.mult)
                      for c in range(2)], reads=xT_b + st_b + [sm_b], writes=xT_b)
        for kc in range(8):
            ps, pb_ = psr.next()

            def fn(e, ps=ps, kc=kc):
                last = None
                for c in range(2):
                    last = e.transpose(out=ps[:, c * 128:(c + 1) * 128],
                                       in_=mem_sb[:, c * 1024 + kc * 128:c * 1024 + (kc + 1) * 128], identity=identF)
                return last
            S.op("pe", fn, reads=xT_b + [cst_b], writes=[pb_])
            copy_op(act_or_dve.next(), memT[:, kc, :], ps[:, 0:256], [pb_], [memT_b])

        pending_ss = []

        def flush_ss(keep=0):
            while len(pending_ss) > keep:
                sq, sq_b, n = pending_ss.pop(0)
                S.op("pe", lambda e, sq=sq, n=n: e.matmul(ss_ps[:], lhsT=onesN[:], rhs=sq, start=(n == 0), stop=(n == 7)),
                     reads=[sq_b, cb_b], writes=[ss_b])

        dummy_b = Buf("dummy_sqrt")

        def preload_sqrt_table():
            S.op("act", lambda e: e.activation(out=small[:, 90:91], in_=eps_col, func=(AF.Ln if RSTD_LN_EXP else AF.Sqrt)),
                 reads=[cst_b], writes=[dummy_b])

        def rstd_from_ss(as_bf16=False):
            rs, rs_b = ftmp.next()
            if RSTD_LN_EXP:
                S.op("act", lambda e: e.activation(out=rs, in_=ss_ps[:], func=AF.Ln, bias=eps_col, scale=1.0),
                     reads=[ss_b, cst_b], writes=[rs_b])
                if as_bf16:
                    rsb, rsb_b = ptr.next()
                    S.op("act", lambda e: e.activation(out=rsb, in_=rs, func=AF.Exp, scale=-0.5), reads=[rs_b], writes=[rsb_b])
                    return rsb, rsb_b
                S.op("act", lambda e: e.activation(out=rs, in_=rs, func=AF.Exp, scale=-0.5), reads=[rs_b], writes=[rs_b])
            else:
                S.op("act", lambda e: e.activation(out=rs, in_=ss_ps[:], func=AF.Sqrt, bias=eps_col, scale=1.0),
                     reads=[ss_b, cst_b], writes=[rs_b])
                S.op("dve", lambda e: e.reciprocal(out=rs, in_=rs), reads=[rs_b], writes=[rs_b])
            return rs, rs_b

        pend_copy = []

        def pre_stat(kc, nxt):
            sq, sq_b = sqb.next()
            S.op("act", lambda e: e.activation(out=sq, in_=xT[:, kc, :], func=AF.Square), reads=[xT_b[kc]], writes=[sq_b])
            pending_ss.append((sq, sq_b, kc))
            flush_ss(keep=2)
            if kc <= 5:
                flush_copy()
            g = G(*nxt)
            pend_copy.append(lambda: S.op("act", lambda e: e.activation(out=hT[:, kc, :], in_=xT[:, kc, :], func=AF.Copy, scale=g[:, kc:kc + 1]),
                                          reads=[xT_b[kc], pa_b], writes=[hT_b[kc]]))

        def flush_copy():
            while pend_copy:
                pend_copy.pop(0)()

        def pre_scale(l, j):
            flush_ss()
            split_next[0] = 2
            rs, rs_b = rstd_from_ss(as_bf16=PRE_RS_BF16)
            flush_copy()
            for kc in range(8):
                eng = "dve"
                S.op(eng, lambda e, kc=kc: e.tensor_tensor(out=hT[:, kc, :], in0=hT[:, kc, :], in1=rs, op=ALU.mult),
                     reads=[hT_b[kc], rs_b], writes=[hT_b[kc]])

        def prenorm(l, j):
            for kc in range(8):
                pre_stat(kc, (l, j))
            pre_scale(l, j)

        def post_chunk(l, j, n, ps, ps_b_):
            y = stage[:, n * T:(n + 1) * T]
            g = G(l, j)
            sq, sq_b = sqb.next()
            S.op("act", lambda e: e.activation(out=sq, in_=ps[:], func=AF.Square), reads=[ps_b_], writes=[sq_b])
            cp_ = lambda: S.op("act", lambda e: e.activation(out=y, in_=ps[:], func=AF.Copy, scale=g[:, n:n + 1]),
                               reads=[ps_b_, pa_b], writes=[st_b[n]])
            if n == 7:
                pend_copy.append(cp_)
            else:
                cp_()
            pending_ss.append((sq, sq_b, n))
            flush_ss(keep=2)

        def post_finish(nxt):
            flush_ss()
            rs, rs_b = rstd_from_ss()
            flush_copy()

            def mul_op(eng, n):
                y = stage[:, n * T:(n + 1) * T]
                S.op(eng, lambda e: e.tensor_tensor(out=y, in0=y, in1=rs, op=ALU.mult), reads=[st_b[n], rs_b], writes=[st_b[n]])

            def add_op(eng, n):
                y = stage[:, n * T:(n + 1) * T]
                S.op(eng, lambda e: e.tensor_tensor(out=xT[:, n, :], in0=xT[:, n, :], in1=y, op=ALU.add),
                     reads=[st_b[n], xT_b[n]], writes=[xT_b[n]])
                if nxt is not None:
                    pre_stat(n, nxt)
            dve_n = [0, 1, 2, 3, 4, 5, 6, 7]
            pool_n = []
            for lst, eng in ((dve_n, "dve"), (pool_n, "pool")):
                if not lst:
                    continue
                mul_op(eng, lst[0])
                for i_, n in enumerate(lst):
                    if i_ + 1 < len(lst):
                        mul_op(eng, lst[i_ + 1])
                    add_op(eng, n)
            if nxt is not None:
                pre_scale(*nxt)

        def proj_fm(w4, wb, ci, rhs_list, rhs_bufs, kcn=8):
            ps, pb_ = psr.next()
            mm_group(ps[:], [(w4[:, ci, kc, :], rhs_list[kc]) for kc in range(kcn)], reads=[wb] + rhs_bufs, writes=[pb_])
            return ps, pb_

        hT_list = [hT[:, kc, :] for kc in range(8)]

        def chk(stage):
            if stop == stage:
                raise _Stop()

        for ti in range(n_tiles):
            r0 = ti * T
            bp32 = bp_t[:].rearrange("p c t -> p (c t)").bitcast(F32)
            if ti == 0:
                S.op("sp", lambda e, r0=r0: e.dma_start(out=stage[:].rearrange("p (b d) -> p b d", b=4),
                                                        in_=x_d[r0:r0 + T, :].rearrange("(b p) d -> p b d", p=128)),
                     writes=st_b, dma_slot="xin0")
                xin, xin_bufs = stage, st_b
            else:
                xin, xin_bufs = bp32, bp_b[0:16]
            for kc in range(8):
                ps, pb_ = psr.next()

                def fn(e, ps=ps, kc=kc, xin=xin):
                    last = None
                    for b in range(4):
                        last = e.transpose(out=ps[:, b * 128:(b + 1) * 128],
                                           in_=xin[:, b * 1024 + kc * 128:b * 1024 + (kc + 1) * 128], identity=identF)
                    return last
                S.op("pe", fn, reads=list(xin_bufs) + [cst_b], writes=[pb_])
                copy_op("dve", xT[:, kc, :], ps[:], [pb_], [xT_b[kc]])
                sq_, sq_b_ = sqb.next()
                S.op("act", lambda e, sq_=sq_, ps=ps: e.activation(out=sq_, in_=ps[:], func=AF.Square), reads=[pb_], writes=[sq_b_])
                pending_ss.append((sq_, sq_b_, kc))
                flush_ss(keep=2)
                g00 = G(0, 0)
                S.op("act", lambda e, ps=ps, kc=kc, g00=g00: e.activation(out=hT[:, kc, :], in_=ps[:], func=AF.Copy, scale=g00[:, kc:kc + 1]),
                     reads=[pb_, pa_b], writes=[hT_b[kc]])
            S.op("sp", lambda e, r0=r0: e.dma_start(out=posi[:], in_=pos_d[0:1, r0:r0 + T].broadcast_to([128, T])),
                 writes=[posi_b], dma_slot="pos")
            a0, a0_b = ftmp.next()
            a1, a1_b = ftmp.next()
            a2, a2_b = ftmp.next()
            ki = posi

            PI = float(np.pi)
            rb = [a0_b, a1_b, a2_b, posi_b]
            steps = [
                lambda e: e.tensor_copy(out=a0, in_=posi[:]),
                lambda e: e.tensor_scalar(out=a0, in0=a0, scalar1=inv_col, scalar2=None, op0=ALU.mult),
                lambda e: e.tensor_scalar(out=a1, in0=a0, scalar1=1.0 / TWO_PI, scalar2=None, op0=ALU.mult),
                lambda e: e.tensor_copy(out=ki[:], in_=a1),
                lambda e: e.tensor_copy(out=a1, in_=ki[:]),
                lambda e: e.scalar_tensor_tensor(out=a0, in0=a1, scalar=-C1, in1=a0, op0=ALU.mult, op1=ALU.add),
                lambda e: e.scalar_tensor_tensor(out=a0, in0=a1, scalar=-C2, in1=a0, op0=ALU.mult, op1=ALU.add),
                lambda e: e.tensor_scalar(out=a0, in0=a0, scalar1=-PI, scalar2=PI, op0=ALU.max, op1=ALU.min),
                lambda e: e.tensor_scalar(out=a1, in0=a0, scalar1=PI / 2, scalar2=None, op0=ALU.add),
                lambda e: e.tensor_scalar(out=a2, in0=a1, scalar1=PI, scalar2=-TWO_PI, op0=ALU.is_gt, op1=ALU.mult),
                lambda e: e.tensor_tensor(out=a1, in0=a1, in1=a2, op=ALU.add),
                lambda e: e.tensor_scalar(out=a1, in0=a1, scalar1=-PI, scalar2=PI, op0=ALU.max, op1=ALU.min),
            ]
            for fn_ in steps:
                S.op("dve", fn_, reads=rb + [cst_b], writes=rb)

            multi("act", [lambda e: e.activation(out=sinT[:], in_=a0, func=AF.Sin, scale=sgn_col),
                          lambda e: e.activation(out=cosT[:], in_=a1, func=AF.Sin)], reads=[a0_b, a1_b, cst_b], writes=[cs_b])

            try:
                chk('rope')
                for l in range(L):
                    if ti == 0:
                        emit_conv(l, "mid")
                        emit_conv(l, "gu")
                    if l == 0:
                        pre_scale(0, 0)
                    chk('prenorm')
                    S.op("pool", lambda e, l=l: e.tensor_copy(out=KT2[:].rearrange("p a h k -> p (a h) k")[:, :, 0:128], in_=Kprev[:, l, :, :]),
                         reads=[Kprev_b[l]], writes=[KT2_b])
                    S.op("pool", lambda e, l=l: e.tensor_copy(out=Vaug[:, 0, :, :], in_=Vprev[:, l, :, :]),
                         reads=[Vprev_b[l]], writes=[Vaug_b])
                    S.op("pool", lambda e, l=l: e.tensor_copy(out=pT[:, :, 0:16], in_=phalo[:, l, :, :]),
                         reads=[phalo_b[l]], writes=[pT_b])

                    wt_ap, wt_b = load_flat(wts[l], 8 * 384, (l, "win"))
                    wt3 = wt_ap[:, 0:8 * 384].rearrange("p (k n) -> p k n", k=8)
                    vg = stage[:, 0:1024].rearrange("p (b c) -> p b c", b=4)
                    ms = small[:, 8:24]
                    ms_b = Buf("ms")
                    for b in range(4):
                        ps, pb_ = psr.next()
                        mm_group(ps[:, 0:384], [(hT[:, kc, b * 128:(b + 1) * 128], wt3[:, kc, :]) for kc in range(8)],
                                 reads=[wt_b] + hT_b, writes=[pb_])
                        S.op("act", lambda e, ps=ps, b=b: e.activation(out=vg[:, b, :], in_=ps[:, 0:256], func=AF.Gelu_apprx_tanh),
                             reads=[pb_], writes=[st_b[b // 2]])
                        S.op("act", lambda e, ps=ps, b=b: e.activation(out=Vaug[:, 1 + b, :, 0:64],
                                                                        in_=ps[:, 256:384].rearrange("p (h d) -> p h d", h=2), func=AF.Copy),
                             reads=[pb_], writes=[Vaug_b])
                        sq, sq_b = ftmp.next()
                        S.op("dve", lambda e, sq=sq, b=b: e.tensor_tensor(out=sq[:, 0:256], in0=vg[:, b, :], in1=vg[:, b, :], op=ALU.mult),
                             reads=[st_b[b // 2]], writes=[sq_b])
                        S.op("dve", lambda e, sq=sq, b=b: e.tensor_reduce(out=ms[:, 4 * b:4 * b + 4],
                                                                          in_=sq[:, 0:256].rearrange("p (g d) -> p g d", g=4),
                                                                          axis=AX.X, op=ALU.add),
                             reads=[sq_b], writes=[ms_b])

                    w4, wb = load_wa(l, 0)
                    for a in range(2):
                        ps, pb_ = proj_fm(w4, wb, 2 + a, hT_list, hT_b)
                        copy_op("act", pT[:, a, 16:16 + T], ps[:], [pb_], [pT_b])

                    A_ = stage[:, 1024:1024 + 2 * (T + 16)].rearrange("p (a t) -> p a t", a=2)
                    B_ = stage[:, 2560:2560 + 2 * (T + 16)].rearrange("p (a t) -> p a t", a=2)
                    A_bufs = [st_b[2], st_b[3], st_b[4]]
                    B_bufs = [st_b[5], st_b[6], st_b[7]]
                    W_ = T + 16
                    ic = invcnt.rearrange("p (a t) -> p a t", a=2)
                    pchain = A_bufs + B_bufs

                    def pstep(fn_, eng="pool"):
                        S.op(eng, fn_, reads=[pT_b, cst_b] + pchain, writes=pchain)

                    def pfin(src, a, p0, p1, invw, ti=ti):
                        if ti == 0:
                            pstep(lambda e: e.tensor_tensor(out=src[p0:p1, a, 16:32], in0=src[p0:p1, a, 16:32], in1=ic[p0:p1, a, :], op=ALU.mult))
                        S.op("dve", lambda e: e.scalar_tensor_tensor(out=pooled[p0:p1, a, :], in0=src[p0:p1, a, 16:W_], scalar=invw,
                                                                      in1=pT[p0:p1, a, 16:W_], op0=ALU.mult, op1=ALU.subtract),
                             reads=[pT_b] + pchain, writes=[pooled_b])
                    pstep(lambda e: e.tensor_tensor(out=A_[:, :, 1:W_], in0=pT[:, :, 1:W_], in1=pT[:, :, 0:W_ - 1], op=ALU.add))
                    pstep(lambda e: e.tensor_tensor(out=B_[64:128, 0, 3:W_], in0=A_[64:128, 0, 3:W_], in1=A_[64:128, 0, 1:W_ - 2], op=ALU.add))
                    pstep(lambda e: e.tensor_tensor(out=B_[:, 1, 3:W_], in0=A_[:, 1, 3:W_], in1=A_[:, 1, 1:W_ - 2], op=ALU.add))
                    pfin(A_, 0, 0, 64, 0.5)
                    pfin(B_, 0, 64, 128, 0.25)
                    pstep(lambda e: e.tensor_tensor(out=A_[:, 1, 7:W_], in0=B_[:, 1, 7:W_], in1=B_[:, 1, 3:W_ - 4], op=ALU.add))
                    pstep(lambda e: e.tensor_tensor(out=B_[64:128, 1, 15:W_], in0=A_[64:128, 1, 15:W_], in1=A_[64:128, 1, 7:W_ - 8], op=ALU.add))
                    pfin(A_, 1, 0, 64, 0.125)
                    pfin(B_, 1, 64, 128, 0.0625)
                    S.op("pool", lambda e, l=l: e.tensor_copy(out=phalo[:, l, :, :], in_=pT[:, :, T:T + 16]),
                         reads=[pT_b], writes=[phalo_b[l]])

                    for a in range(2):
                        ps, pb_ = proj_fm(w4, wb, a, hT_list, hT_b)
                        S.op("act", lambda e, ps=ps, a=a: e.activation(out=uT[:, a, :], in_=ps[:], func=AF.Gelu_apprx_tanh),
                             reads=[pb_], writes=[uT_b[a]])

                    rv = small[:, 24:40]
                    rv_b = Buf("rv")
                    S.op("act", lambda e: e.activation(out=rv, in_=ms, func=AF.Sqrt, bias=eps_col, scale=1.0 / 64),
                         reads=[ms_b, cst_b], writes=[rv_b])
                    S.op("dve", lambda e: e.reciprocal(out=rv, in_=rv), reads=[rv_b], writes=[rv_b])
                    for b in range(4):
                        base = vn_pad[:, b, 0, :]
                        vo = bass.AP(tensor=base.tensor, offset=base.offset, ap=[list(base.ap[0]), [256, 2], [192, 2], [1, 64]])
                        S.op("dve", lambda e, b=b, vo=vo: e.tensor_tensor(
                            out=vo, in0=vg[:, b, :].rearrange("p (a c d) -> p a c d", a=2, c=2),
                            in1=rv[:, 4 * b:4 * b + 4].rearrange("p (a c o) -> p a c o", a=2, o=1).broadcast_to([128, 2, 2, 64]),
                            op=ALU.mult), reads=[st_b[b // 2], rv_b], writes=[vn_b[b]])

                    def rope_pair(w4, wb, ci, outs, out_b):
                        ps1, pb1 = proj_fm(w4, wb, ci, hT_list, hT_b)
                        ps2, pb2 = proj_fm(w4, wb, ci + 1, hT_list, hT_b)
                        t1, t1_b = ftmp.next()
                        t2, t2_b = ftmp.next()
                        S.op("dve", lambda e: e.tensor_tensor(out=t1, in0=ps1[:], in1=cosT[:], op=ALU.mult),
                             reads=[pb1, cs_b], writes=[t1_b])
                        S.op("dve", lambda e: e.tensor_tensor(out=t2, in0=ps2[:], in1=sinT[:], op=ALU.mult),
                             reads=[pb2, cs_b], writes=[t2_b])
                        for (o_ap, p0, p1) in outs:
                            S.op("pool", lambda e, o_ap=o_ap, p0=p0, p1=p1: e.tensor_tensor(out=o_ap, in0=t1[p0:p1, :], in1=t2[p0:p1, :],
                                                                                         op=ALU.add),
                                 reads=[t1_b, t2_b], writes=[out_b])
                    w4, wb = load_wa(l, 12)
                    for h in range(2):
                        rope_pair(w4, wb, 2 * h, [(KT2[0:64, 0, h, 128:128 + T], 0, 64), (KT2[64:128, 1, h, 128:128 + T], 64, 128)], KT2_b)
                    for grp in range(2):
                        w4, wb = load_wa(l, 4 + 4 * grp)
                        for jj in range(2):
                            j = grp * 2 + jj
                            rope_pair(w4, wb, 2 * jj, [(QT[:, j, :], 0, 128)], QT_b[j])
                    chk('wintm')

                    for a in range(2):
                        ps, pb_ = psr.next()

                        def gm_mm(e, ps=ps, a=a, l=l):
                            last = None
                            for b in range(4):
                                for c in range(2):
                                    g = 2 * a + c
                                    last = e.matmul(ps[:, b * 128:(b + 1) * 128], lhsT=vn_pad[:, b, g, :], rhs=WsT[:, l, g, :],
                                                    start=(c == 0), stop=(c == 1))
                            return last
                        S.op("pe", gm_mm, reads=vn_b + [wsbd_b], writes=[pb_])
                        tmp, tmp_b = ftmp.next()
                        S.op("dve", lambda e, ps=ps, a=a, l=l, tmp=tmp: e.scalar_tensor_tensor(
                            out=tmp.rearrange("p (b t) -> p b t", b=4), in0=ps[:].rearrange("p (b t) -> p b t", b=4),
                            scalar=gmg(l)[:, a:a + 1],
                            in1=bfull(l, a).rearrange("p (o t) -> p o t", o=1).broadcast_to([128, 4, 128]),
                            op0=ALU.mult, op1=ALU.add), reads=[pb_, pa_b], writes=[tmp_b])
                        S.op("pool", lambda e, a=a, tmp=tmp: e.tensor_tensor(out=bp_t[:, a, :], in0=tmp, in1=uT[:, a, :], op=ALU.mult),
                             reads=[tmp_b, uT_b[a]], writes=[bp_b[a]])
                    chk('gm')
                    for a in range(2):
                        ps, pb_ = psr.next()
                        S.op("pe", lambda e, ps=ps, a=a, l=l: e.matmul(ps[:], lhsT=BD[:, l, a, :], rhs=pooled[:, a, :], start=True, stop=True),
                             reads=[pooled_b, wsbd_b], writes=[pb_])
                        S.op("act", lambda e, ps=ps, a=a, l=l: e.activation(out=bp_t[:, 2 + a, :], in_=ps[:], func=AF.Copy,
                                                                            scale=psc(l)[:, a:a + 1]),
                             reads=[pb_, pa_b], writes=[bp_b[2 + a]])
                    chk('pool')

                    its = [(n, h) for n in range(NB) for h in range(2)]
                    cts = {}
                    ctx = {}

                    def swa_A(i):
                        n, h = its[i]
                        has_prev = (ti * NB + n) > 0
                        c = {"n": n, "h": h, "has_prev": has_prev}
                        if h == 0:
                            cts[n] = ctr.next()

                        def s_mm(e, ps, koff, h=h, n=n):
                            e.matmul(ps[:], lhsT=identB[:], rhs=mbias[:, 0 if koff else 1, :], start=True, stop=False)
                            last = None
                            for g in range(4):
                                last = e.matmul(ps[:, g * 128:(g + 1) * 128],
                                                lhsT=KT2[:, g % 2, h, koff + n * 128:koff + (n + 1) * 128],
                                                rhs=QT[:, 2 * h + g // 2, n * 128:(n + 1) * 128], start=False, stop=(g == 3))
                            return last
                        c["psc"], c["psc_b"] = psr.next()
                        S.op("pe", lambda e, ps=c["psc"]: s_mm(e, ps, 128), reads=[KT2_b, cb_b] + QT_b, writes=[c["psc_b"]])
                        if has_prev:
                            c["psp"], c["psp_b"] = psr.next()
                            S.op("pe", lambda e, ps=c["psp"]: s_mm(e, ps, 0), reads=[KT2_b, cb_b] + QT_b, writes=[c["psp_b"]])
                        ctx[i] = c

                    def swa_B(i):
                        c = ctx[i]
                        for key in ("c", "p"):
                            if key == "p" and not c["has_prev"]:
                                continue
                            pt, pt_b = ptr.next()
                            ps, ps_b = c["ps" + key], c["ps" + key + "_b"]
                            c["pt" + key], c["pt" + key + "_b"] = pt, pt_b
                            S.op("act", lambda e, ps=ps, pt=pt: e.activation(out=pt, in_=ps[:], func=AF.Exp, scale=0.125),
                                 reads=[ps_b], writes=[pt_b])

                    def swa_C(i):
                        c = ctx[i]
                        n, h, has_prev = c["n"], c["h"], c["has_prev"]
                        po, po_b = psr.next()
                        pov = po[:, 0:260].rearrange("p (g d) -> p g d", g=4)
                        c["pov"], c["po_b"] = pov, po_b
                        ptc = c["ptc"]
                        ptp = c.get("ptp")

                        def pv_mm(e):
                            last = None
                            for g in range(4):
                                if has_prev:
                                    e.matmul(pov[:, g, :], lhsT=ptp[:, g * 128:(g + 1) * 128], rhs=Vaug[:, n, h, 0:65], start=True, stop=False)
                                last = e.matmul(pov[:, g, :], lhsT=ptc[:, g * 128:(g + 1) * 128], rhs=Vaug[:, n + 1, h, 0:65],
                                                start=(not has_prev), stop=True)
                            return last
                        S.op("pe", pv_mm, reads=[c["ptc_b"], Vaug_b] + ([c["ptp_b"]] if has_prev else []), writes=[po_b])

                    def swa_D(i, l=l):
                        c0, c1 = ctx[i - 1], ctx[i]
                        n = c1["n"]
                        ct, ct_b = cts[n]
                        o_ = 40 + 16 * ((i // 2) % 2)
                        dt_ = small[:, o_:o_ + 8]
                        rc_ = small[:, o_ + 8:o_ + 16]
                        dt_b = Buf("dt")
                        for c in (c0, c1):
                            h = c["h"]
                            S.op("dve", lambda e, c=c, h=h: e.tensor_tensor(
                                out=dt_[:, 4 * h:4 * h + 4].rearrange("p (g o) -> p g o", o=1), in0=c["pov"][:, :, 64:65],
                                in1=esink[:, l * 8 + 4 * h:l * 8 + 4 * h + 4].rearrange("p (g o) -> p g o", o=1), op=ALU.add),
                                reads=[c["po_b"], esink_b], writes=[dt_b])
                        S.op("act", lambda e: e.activation(out=rc_, in_=dt_, func=AF.Ln), reads=[dt_b], writes=[dt_b])
                        S.op("act", lambda e: e.activation(out=rc_, in_=rc_, func=AF.Exp, scale=-1.0), reads=[dt_b], writes=[dt_b])
                        for c in (c0, c1):
                            h = c["h"]
                            S.op("dve", lambda e, c=c, h=h: e.tensor_tensor(
                                out=ct[:, h * 256:(h + 1) * 256].rearrange("p (g d) -> p g d", g=4), in0=c["pov"][:, :, 0:64],
                                in1=rc_[:, 4 * h:4 * h + 4].rearrange("p (g o) -> p g o", o=1).broadcast_to([128, 4, 64]), op=ALU.mult),
                                reads=[c["po_b"], dt_b], writes=[ct_b])

                    def swa_E(n):
                        ct, ct_b = cts[n]
                        pst, pst_b = psr.next()
                        pstb = pst[:].bitcast(BF16)

                        def tr4(e):
                            last = None
                            for j in range(4):
                                last = e.transpose(out=pstb[:, j * 128:(j + 1) * 128], in_=ct[:, j * 128:(j + 1) * 128], identity=identB[:])
                            return last
                        S.op("pe", tr4, reads=[ct_b, cb_b], writes=[pst_b])
                        copy_op("act", bp_t[:, 4:8, n * 128:(n + 1) * 128],
                                pstb[:, 0:512].rearrange("p (j q) -> p j q", j=4), [pst_b], bp_b[4:8])

                    swa_A(0)
                    for i in range(len(its)):
                        if i + 1 < len(its):
                            swa_A(i + 1)
                        swa_B(i)
                        swa_C(i)
                        if i % 2 == 1:
                            swa_D(i)
                        if i % 2 == 0 and i >= 2:
                            swa_E((i - 2) // 2)
                    swa_E(NB - 1)
                    S.op("pool", lambda e, l=l: e.tensor_copy(out=Kprev[:, l, :, :], in_=KT2[:].rearrange("p a h k -> p (a h) k")[:, :, T:T + 128]),
                         reads=[KT2_b], writes=[Kprev_b[l]])
                    S.op("pool", lambda e, l=l: e.tensor_copy(out=Vprev[:, l, :, :], in_=Vaug[:, 4, :, :]),
                         reads=[Vaug_b], writes=[Vprev_b[l]])

                    preload_sqrt_table()
                    chk('swa')
                    mix_list = [bp_t[:, kc, :] for kc in range(8)]
                    for grp in range(2):
                        w4, wb = load_wa(l, OFF_WO + 4 * grp)
                        for ci in range(4):
                            ps, pb_ = proj_fm(w4, wb, ci, mix_list, bp_b[0:8])
                            flush_ss()
                            post_chunk(l, 1, grp * 4 + ci, ps, pb_)
                    post_finish((l, 2))

                    chk('wo')
                    if ti == 0:
                        emit_conv(l, "dn")
                    if ti == 0:
                        mem_list = [memT[:, kc, :] for kc in range(8)]
                        for grp in range(2):
                            w4, wb = load_wa(l, OFF_XK + 4 * grp)
                            for ci in range(4):
                                ps, pb_ = psr.next()
                                mm_group(ps[:, 0:256], [(w4[:, ci, kc, :], mem_list[kc]) for kc in range(8)], reads=[wb, memT_b], writes=[pb_])
                                copy_op(act_or_dve.next(), KmT[:, l, grp * 4 + ci, :], ps[:, 0:256], [pb_], [KmT_b[l]])
                        for half in range(2):
                            wv_ap, wv_b = load_flat(wvs[l, half], 4096, (l, "mid"))
                            wv3 = wv_ap.rearrange("p (k n) -> p k n", k=8)
                            for mc in range(2):
                                ps, pb_ = psr.next()
                                mm_group(ps[:], [(memT[:, kc, mc * 128:(mc + 1) * 128], wv3[:, kc, :]) for kc in range(8)],
                                         reads=[wv_b, memT_b], writes=[pb_])
                                copy_op(act_or_dve.next(), Vm[:, l, mc, half * 512:(half + 1) * 512], ps[:], [pb_], [Vm_b[l]])
                    for grp in range(2):
                        w4, wb = load_wa(l, OFF_XQ + 4 * grp)
                        for ci in range(4):
                            n = grp * 4 + ci
                            ps, pb_ = proj_fm(w4, wb, ci, hT_list, hT_b)
                            copy_op(act_or_dve.next(), bp_t[:, 8 + n, :], ps[:], [pb_], [bp_b[8 + n]])
                    xc = {}

                    def xa_A(hx, l=l):
                        c = {"ps": []}
                        for mc in range(2):
                            ps, pb_ = psr.next()
                            mm_group(ps[:], [(KmT[:, l, 2 * hx + dc, mc * 128:(mc + 1) * 128], bp_t[:, 8 + 2 * hx + dc, :]) for dc in range(2)],
                                     reads=[KmT_b[l], bp_b[8 + 2 * hx], bp_b[9 + 2 * hx]], writes=[pb_])
                            c["ps"].append((ps, pb_))
                        xc[hx] = c

                    def xa_B(hx):
                        c = xc[hx]
                        c["px"] = []
                        for mc in range(2):
                            ps, pb_ = c["ps"][mc]
                            px, px_b = pxr.next()
                            S.op("act", lambda e, ps=ps, px=px: e.activation(out=px, in_=ps[:], func=AF.Exp, scale=1.0 / 16),
                                 reads=[pb_], writes=[px_b])
                            c["px"].append((px, px_b))

                    def xa_C(hx, l=l):
                        c = xc[hx]
                        pxs = c["px"]
                        pd, pd_b = psr.next()
                        mm_group(pd[:], [(ones1[:], pxs[mc][0]) for mc in range(2)], reads=[cb_b, pxs[0][1], pxs[1][1]], writes=[pd_b])
                        c["pd"] = (pd, pd_b)
                        c["po"] = []
                        for dc in range(2):
                            ps, pb_ = psr.next()
                            mm_group(ps[:], [(Vm[:, l, mc, hx * 256 + dc * 128:hx * 256 + (dc + 1) * 128], pxs[mc][0]) for mc in range(2)],
                                     reads=[Vm_b[l], pxs[0][1], pxs[1][1]], writes=[pb_])
                            c["po"].append((ps, pb_))

                    def xa_D(hx):
                        c = xc[hx]
                        pd, pd_b = c["pd"]
                        rc, rc_b = ftmp.next()
                        if RSTD_LN_EXP:
                            S.op("act", lambda e: e.activation(out=rc, in_=pd[:], func=AF.Ln), reads=[pd_b], writes=[rc_b])
                            S.op("act", lambda e: e.activation(out=rc, in_=rc, func=AF.Exp, scale=-1.0), reads=[rc_b], writes=[rc_b])
                        else:
                            S.op("dve", lambda e: e.reciprocal(out=rc, in_=pd[:]), reads=[pd_b], writes=[rc_b])
                        for dc in range(2):
                            ps, pb_ = c["po"][dc]
                            S.op("dve", lambda e, ps=ps, dc=dc: e.tensor_tensor(out=bp_t[:, 16 + 2 * hx + dc, :], in0=ps[:], in1=rc, op=ALU.mult),
                                 reads=[pb_, rc_b], writes=[bp_b[16 + 2 * hx + dc]])
                    xa_A(0)
                    for hx in range(4):
                        xa_B(hx)
                        if hx + 1 < 4:
                            xa_A(hx + 1)
                        xa_C(hx)
                        xa_D(hx)
                    preload_sqrt_table()
                    ox_list = [bp_t[:, 16 + kc, :] for kc in range(8)]
                    for grp in range(2):
                        w4, wb = load_wa(l, OFF_XO + 4 * grp)
                        for ci in range(4):
                            ps, pb_ = proj_fm(w4, wb, ci, ox_list, bp_b[16:24])
                            flush_ss()
                            post_chunk(l, 3, grp * 4 + ci, ps, pb_)
                    post_finish((l, 4))

                    chk('xattn')
                    if ti == 0 and l + 1 < L:
                        emit_conv(l + 1, "win")
                    for grp in range(11):
                        w4, wb = load_wa(l, OFF_GU + 4 * grp)
                        for jj in range(2):
                            j = grp * 2 + jj
                            psg, pbg = proj_fm(w4, wb, 2 * jj, hT_list, hT_b)
                            psu, pbu = proj_fm(w4, wb, 2 * jj + 1, hT_list, hT_b)
                            sg, sg_b = ftmp.next()
                            S.op("act", lambda e, psg=psg, sg=sg: e.activation(out=sg, in_=psg[:], func=AF.Silu), reads=[pbg], writes=[sg_b])
                            S.op("dve", lambda e, psu=psu, sg=sg, j=j: e.tensor_tensor(out=bp_t[:, j, :], in0=psu[:], in1=sg, op=ALU.mult),
                                 reads=[pbu, sg_b], writes=[bp_b[j]])
                    preload_sqrt_table()
                    g_list = [bp_t[:, j, :] for j in range(22)]
                    for n in range(8):
                        wd_ap, wd_b = load_flat(wds[l, n], 2816, (l, "dn"))
                        wd3 = wd_ap[:, 0:2816].rearrange("p (k j) -> p k j", k=22)
                        ps, pb_ = psr.next()
                        mm_group(ps[:], [(wd3[:, j, :], g_list[j]) for j in range(22)], reads=[wd_b] + bp_b[0:22], writes=[pb_])
                        flush_ss()
                        post_chunk(l, 5, n, ps, pb_)
                    if l + 1 == L and ti + 1 < n_tiles:
                        S.op("pool", lambda e, r1=r0 + T: e.dma_start(out=bp32[:, 0:4096].rearrange("p (b d) -> p b d", b=4),
                                                                      in_=x_d[r1:r1 + T, :].rearrange("(b p) d -> p b d", p=128)),
                             writes=bp_b[0:16], dma_slot="xin")
                    post_finish((l + 1, 0) if l + 1 < L else None)
            except _Stop:
                S.op('sp', lambda e: e.dma_start(out=small[:, 56:64], in_=cst_d[:, 539:547]), reads=[], writes=[Buf('dummy')], dma_slot='dummy')

            for b in range(4):
                for jh in range(2):
                    ps, pb_ = psr.next()

                    def fn(e, ps=ps, b=b, jh=jh):
                        last = None
                        for q in range(4):
                            kc = jh * 4 + q
                            last = e.transpose(out=ps[:, q * 128:(q + 1) * 128], in_=xT[:, kc, b * 128:(b + 1) * 128], identity=identF)
                        return last
                    S.op("pe", fn, reads=xT_b[jh * 4:jh * 4 + 4] + [cst_b], writes=[pb_])
                    copy_op(act_or_dve.next(), stage[:, b * 1024 + jh * 512:b * 1024 + (jh + 1) * 512], ps[:], [pb_], [st_b[2 * b + jh]])
            S.op("pool", lambda e, r0=r0: e.dma_start(out=out_d[r0:r0 + T, :].rearrange("(b p) d -> p b d", p=128),
                                                      in_=stage[:].rearrange("p (b d) -> p b d", b=4)),
                 reads=st_b, writes=[], dma_slot="xout")

        nout = n_tiles

        def fin(e):
            for nm_, sem_ in S.slot_sem.items():
                if S.slot_cnt[nm_] > 0:
                    e.wait_ge(sem_, S.slot_cnt[nm_])
            return None
        S.ops["pool"].append(_mk_plain(fin))
        print('sbuf remaining', nc.sbuf_bytes_remaining, 'sems free', nc.free_len())
        S.emit()
    return nc


def _mk_plain(fn):
    o = Op()
    o.eng = "pool"
    o.fn = lambda e: fn(e)
    o.is_dma = False
    o.sig = False
    o.waits = []
    o.deps = []
    o.ndma = 0
    o.sem = None
    o.val = None
    return o


def _consts():
    c = np.zeros((128, NCONST), np.float32)
    i = np.arange(128)
    c[:, 0:128] = np.eye(128, dtype=np.float32)
    c[:, 128:256] = (i[:, None] <= i[None, :])
    c[:, 256:384] = (i[:, None] <= i[None, :])
    c[:, 384:512] = (i[:, None] > i[None, :])
    half = 32
    inv = (np.float32(10000.0) ** (-np.arange(half, dtype=np.float32) / np.float32(half))).astype(np.float32)
    c[:, 512] = inv[i % 32]
    c[:, 513] = np.where((i % 64) < 32, -1.0, 1.0)
    wins = (2, 4, 8, 16)
    t = np.arange(16)
    for a in range(2):
        for p in range(128):
            w = wins[2 * a + p // 64]
            c[p, 514 + a * 16:514 + (a + 1) * 16] = float(w) / np.minimum(t + 1, w)
    c[:, 546] = EPS
    return c


def _chunks(w, cols):
    K = w.shape[0]
    kc = K // 128
    ws = w[:, cols]
    n = ws.shape[1] // 128
    return np.ascontiguousarray(ws.reshape(kc, 128, n, 128).transpose(2, 1, 0, 3)).reshape(n, 128, kc * 128)


def prep_shared(inp, depth=DEPTH):
    L = depth
    f = np.float32
    def qcols(j, rot):
        cols = []
        for h in (2 * j, 2 * j + 1):
            base = 768 + h * 64
            if rot:
                cols += list(range(base + 32, base + 64)) + list(range(base, base + 32))
            else:
                cols += list(range(base, base + 64))
        return cols

    def kcols(h, rot):
        base = 1280 + h * 64
        one = (list(range(base + 32, base + 64)) + list(range(base, base + 32))) if rot else list(range(base, base + 64))
        return one + one
    fm = list(range(0, 256)) + list(range(512, 768))
    for j in range(4):
        fm += qcols(j, False) + qcols(j, True)
    for h in range(2):
        fm += kcols(h, False) + kcols(h, True)
    fm = np.array(fm)
    tm = np.array(list(range(256, 512)) + list(range(1408, 1536)))
    gu = []
    for j in range(22):
        gu += list(range(j * 128, (j + 1) * 128)) + list(range(DFF + j * 128, DFF + (j + 1) * 128))
    gu = np.array(gu)
    allc = np.arange(1024)
    wa = np.empty((L, NWA, 128, 1024), f)
    wd = np.empty((L, 8, 128, 2816), f)
    wt = np.empty((L, 128, 8 * 384), f)
    wv = np.empty((L, 2, 128, 4096), f)
    for l in range(L):
        wa[l, 0:16] = _chunks(inp["w_in"][l], fm)
        wa[l, 16:24] = _chunks(inp["w_o"][l], allc)
        wa[l, 24:32] = _chunks(inp["w_xq"][l], allc)
        wa[l, 32:40] = _chunks(inp["w_xo"][l], allc)
        wa[l, 40:84] = _chunks(inp["w_gate_up"][l], gu)
        wa[l, 84:92] = _chunks(inp["w_xkv"][l], allc)
        wd[l] = _chunks(inp["w_down"][l], allc)
        wt[l] = inp["w_in"][l][:, tm].reshape(8, 128, 384).transpose(1, 0, 2).reshape(128, 8 * 384)
        for h in range(2):
            wv[l, h] = inp["w_xkv"][l][:, 1024 + h * 512:1024 + (h + 1) * 512].reshape(8, 128, 512).transpose(1, 0, 2).reshape(128, 4096)
    pa = np.zeros((128, NPA), f)
    gs = [inp[k] for k in ("mix_pre_g", "mix_post_g", "x_pre_g", "x_post_g", "ffn_pre_g", "ffn_post_g")]
    for l in range(L):
        for j in range(6):
            o = (l * 6 + j) * 8
            pa[:, o:o + 8] = gs[j][l].reshape(8, 128).T
        pa[:, 192 + 2 * l:194 + 2 * l] = inp["gm_v_g"][l].reshape(2, 128).T
        pa[:, 200 + 2 * l:202 + 2 * l] = inp["pool_scale"][l].reshape(2, 128).T
        pa[:, 208 + 8 * l:216 + 8 * l] = inp["attn_sinks"][l][None, :]
        bf = np.repeat(inp["gm_b_s"][l], 64, axis=0).reshape(2, 128, 128)
        for a in range(2):
            o = 240 + (l * 2 + a) * 128
            pa[:, o:o + 128] = bf[a]
    pb = np.zeros((128, NPB), f)
    for l in range(L):
        for g in range(4):
            o = (l * 4 + g) * 128
            pb[:, o:o + 128] = inp["gm_w_s"][l, g].T
        for a in range(2):
            o = 2048 + (l * 2 + a) * 128
            pb[0:64, o:o + 64] = inp["pool_w"][l, 2 * a]
            pb[64:128, o + 64:o + 128] = inp["pool_w"][l, 2 * a + 1]
    pb[:, 3072:4096] = inp["mem_norm_g"][None, :]
    return {"cst": _consts(), "pa": pa, "pb": pb, "wa": wa, "wd": wd, "wt": wt, "wv": wv}


_CACHE = {}


def kernel(**inputs):
    inp = {k: np.asarray(v) for k, v in inputs.items()}
    B = inp["x"].shape[0]
    shared = prep_shared(inp)
    if "nc" not in _CACHE:
        _CACHE["nc"] = build_program()
    nc = _CACHE["nc"]
    in_maps = []
    for b in range(B):
        m = dict(shared)
        m["x"] = np.ascontiguousarray(inp["x"][b])
        m["mem"] = np.ascontiguousarray(inp["mem"][b])
        m["pos"] = np.ascontiguousarray(inp["positions"][b].astype(np.int32)).reshape(1, -1)
        in_maps.append(m)
    res = run_bass_kernel_spmd(nc, in_maps, core_ids=list(range(B)))
    return np.stack([r["out"] for r in res.results], axis=0).astype(np.float32)
```

```python
import numpy as np
from contextlib import ExitStack
import concourse.bass as bass
import concourse.mybir as mybir
from concourse.bass_utils import run_bass_kernel_spmd

F32 = mybir.dt.float32
BF16 = mybir.dt.bfloat16
I32 = mybir.dt.int32
AF = mybir.ActivationFunctionType
ALU = mybir.AluOpType
AX = mybir.AxisListType

D = 1024
SEQ = 4096
DEPTH = 4
MEM = 256
DFF = 2816
T = 512
NB = 4
EPS = 1e-6
PRE_RS_BF16 = False
RSTD_LN_EXP = True
NWA = 92
OFF_WIN, OFF_WO, OFF_XQ, OFF_XO, OFF_GU, OFF_XK = 0, 16, 24, 32, 40, 84
NPA = 192 + 8 + 8 + 32 + 1024
NPB = 2048 + 1024 + 1024
NCONST = 128 * 4 + 1 + 1 + 32 + 1
TWO_PI = float(2 * np.pi)
C1 = 6.28125
C2 = float(2 * np.pi - 6.28125)


class Buf:
    __slots__ = ("name", "last_w", "readers", "excl")

    def __init__(self, name, excl=False):
        self.name = name
        self.last_w = None
        self.readers = []
        self.excl = excl


class Op:
    __slots__ = ("eng", "fn", "deps", "sig", "sem", "val", "is_dma", "waits", "clock", "idx", "ndma")


class Sched:
    ENGS = ("pe", "act", "dve", "pool", "sp")

    def __init__(self, nc, stack):
        self.nc = nc
        self.stack = stack
        self.ops = {e: [] for e in self.ENGS}
        self.all = []
        self.eng_sem = {}
        for e in ("pe", "act", "dve", "pool"):
            self.eng_sem[e] = stack.enter_context(nc.semaphore("s_" + e))
        self.slot_sem = {}
        self.slot_cnt = {}

    def _slot(self, slot):
        if slot not in self.slot_sem:
            self.slot_sem[slot] = self.stack.enter_context(self.nc.semaphore("d_" + slot))
            self.slot_cnt[slot] = 0
        return self.slot_sem[slot]

    def op(self, eng, fn, reads=(), writes=(), dma_slot=None, ndma=1):
        o = Op()
        o.eng = eng
        o.fn = fn
        o.is_dma = dma_slot is not None
        o.ndma = ndma
        o.sig = False
        o.waits = []
        o.idx = len(self.all)
        ex = [b for b in reads if b.excl]
        if ex:
            reads = [b for b in reads if not b.excl]
            writes = list(writes) + [b for b in ex if b not in writes]
        deps = []
        for b in reads:
            if b.last_w is not None:
                deps.append((b.last_w, 0))
        for b in writes:
            if b.last_w is not None:
                deps.append((b.last_w, 1))
            for r in b.readers:
                deps.append((r, 1))
        o.deps = []
        seen = set()
        for p, kind in deps:
            if p is o or id(p) in seen:
                continue
            if (not o.is_dma) and (not p.is_dma) and p.eng == eng and kind != 0 and eng == "pe":
                continue
            seen.add(id(p))
            o.deps.append(p)
            p.sig = True
        if o.is_dma:
            o.sem = self._slot(dma_slot)
            self.slot_cnt[dma_slot] += 16 * ndma
            o.val = self.slot_cnt[dma_slot]
            o.sig = True
        else:
            o.sem = self.eng_sem.get(eng)
            o.val = None
        for b in writes:
            b.last_w = o
            b.readers = []
        for b in reads:
            if b.last_w is not o:
                b.readers.append(o)
        self.ops[eng].append(o)
        self.all.append(o)
        return o

    def finalize(self):
        cnt = {e: 0 for e in self.eng_sem}
        for o in self.all:
            if not o.is_dma and o.sig:
                cnt[o.eng] += 1
                o.val = cnt[o.eng]
        clock = {e: {} for e in self.ENGS}
        for o in self.all:
            ck = clock[o.eng]
            for p in sorted(o.deps, key=lambda p: -p.idx):
                key = id(p.sem)
                if ck.get(key, 0) >= p.val:
                    continue
                o.waits.append((p.sem, p.val))
                for k, v in p.clock.items():
                    if ck.get(k, 0) < v:
                        ck[k] = v
                if ck.get(key, 0) < p.val:
                    ck[key] = p.val
            o.clock = dict(ck)
            if o.sig:
                o.clock[id(o.sem)] = max(o.clock.get(id(o.sem), 0), o.val)

    def emit(self):
        self.finalize()
        nc = self.nc

        def run(e, ops):
            for o in ops:
                for sem, val in o.waits:
                    e.wait_ge(sem, val)
                ins = o.fn(e)
                if o.is_dma:
                    lst = ins if isinstance(ins, (list, tuple)) else [ins]
                    assert len(lst) == o.ndma
                    for i_ in lst:
                        i_.then_inc(o.sem, 16)
                elif o.sig:
                    ins.then_inc(o.sem, 1)

        with nc.Block() as block:
            @block.tensor
            def _(e):
                run(e, self.ops["pe"])

            @block.scalar
            def _(e):
                run(e, self.ops["act"])

            @block.vector
            def _(e):
                run(e, self.ops["dve"])

            @block.gpsimd
            def _(e):
                run(e, self.ops["pool"])

            @block.sync
            def _(e):
                run(e, self.ops["sp"])


class Rot:
    def __init__(self, items):
        self.items = items
        self.i = 0

    def next(self):
        it = self.items[self.i % len(self.items)]
        self.i += 1
        return it


class _Stop(Exception):
    pass


def build_program(n_tiles=SEQ // T, depth=DEPTH, stop=None):
    nc = bass.Bass("TRN2", target_bir_lowering=False)
    seq = n_tiles * T
    L = depth

    def din(name, shape, dt=F32):
        return nc.dram_tensor(name, list(shape), dt, kind="ExternalInput").ap()

    x_d = din("x", [seq, D])
    mem_d = din("mem", [MEM, D])
    pos_d = din("pos", [1, seq], I32)
    cst_d = din("cst", [128, NCONST])
    pa_d = din("pa", [128, NPA])
    pb_d = din("pb", [128, NPB])
    wa_d = din("wa", [L, NWA, 128, 1024])
    wd_d = din("wd", [L, 8, 128, 2816])
    wt_d = din("wt", [L, 128, 8 * 384])
    wv_d = din("wv", [L, 2, 128, 4096])
    out_d = nc.dram_tensor("out", [seq, D], F32, kind="ExternalOutput").ap()
    was = nc.dram_tensor("was", [L, NWA, 128, 1024], BF16, kind="Internal").ap()
    wds = nc.dram_tensor("wds", [L, 8, 128, 2816], BF16, kind="Internal").ap()
    wts = nc.dram_tensor("wts", [L, 128, 8 * 384], BF16, kind="Internal").ap()
    wvs = nc.dram_tensor("wvs", [L, 2, 128, 4096], BF16, kind="Internal").ap()

    st = ExitStack()
    with st:
        S = Sched(nc, st)

        def sb(name, shape, dt):
            return st.enter_context(nc.sbuf_tensor("sb_" + name, list(shape), dt))

        def psum(name, shape, dt):
            return st.enter_context(nc.psum_tensor("pp_" + name, list(shape), dt))

        xT = sb("xT", [128, 8, T], F32)
        xT_b = [Buf("xT%d" % k) for k in range(8)]
        stage = sb("stage", [128, 4096], F32)
        st_b = [Buf("st%d" % k) for k in range(8)]
        hT = sb("hT", [128, 8, T], BF16)
        hT_b = [Buf("hT%d" % k) for k in range(8)]
        sqb_t = sb("sqb", [128, 4, T], BF16)
        sqb = Rot([(sqb_t[:, i, :], Buf("sqb%d" % i)) for i in range(4)])
        NF = 4
        ft_t = sb("ftmp", [128, NF, T], F32)
        ftmp = Rot([(ft_t[:, i, :], Buf("ft%d" % i)) for i in range(NF)])
        bp_t = sb("bpool", [128, 24, T], BF16)
        bp_b = [Buf("bp%d" % k) for k in range(24)]
        uT = sb("uT", [128, 2, T], F32)
        uT_b = [Buf("uT0"), Buf("uT1")]
        pT = sb("pT", [128, 2, T + 16], F32)
        pT_b = Buf("pT")
        pooled = sb("pooled", [128, 2, T], BF16)
        pooled_b = Buf("pooled")
        QT = sb("QT", [128, 4, T], BF16)
        QT_b = [Buf("QT%d" % k) for k in range(4)]
        KT2 = sb("KT2", [128, 2, 2, 128 + T], BF16)
        KT2_b = Buf("KT2")
        Kprev = sb("Kprev", [128, L, 4, 128], BF16)
        Kprev_b = [Buf("Kprev%d" % l) for l in range(L)]
        Vaug = sb("Vaug", [128, 5, 2, 80], BF16)
        Vaug_b = Buf("Vaug")
        Vprev = sb("Vprev", [128, L, 2, 80], BF16)
        Vprev_b = [Buf("Vprev%d" % l) for l in range(L)]
        phalo = sb("phalo", [128, L, 2, 16], F32)
        phalo_b = [Buf("phalo%d" % l) for l in range(L)]
        cosT = sb("cosT", [128, T], F32)
        sinT = sb("sinT", [128, T], F32)
        cs_b = Buf("cossin")
        vn_pad = sb("vn_pad", [128, 4, 4, 128], BF16)
        vn_b = [Buf("vn%d" % k) for k in range(4)]
        pt_t = sb("PT", [128, 4, T], BF16)
        ptr = Rot([(pt_t[:, i, :], Buf("PT%d" % i)) for i in range(4)])
        ct_t = sb("ctok", [128, 2, T], BF16)
        ctr = Rot([(ct_t[:, i, :], Buf("ctok%d" % i)) for i in range(2)])
        px_t = sb("PxT", [128, 4, T], BF16)
        pxr = Rot([(px_t[:, i, :], Buf("PxT%d" % i)) for i in range(4)])
        memT = sb("memT", [128, 8, MEM], BF16)
        memT_b = Buf("memT")
        KmT = sb("KmT", [128, L, 8, MEM], BF16)
        KmT_b = [Buf("KmT%d" % l) for l in range(L)]
        Vm = sb("Vm", [128, L, 2, D], BF16)
        Vm_b = [Buf("Vm%d" % l) for l in range(L)]
        pa = sb("pa", [128, NPA], F32)
        pa_b = Buf("pa")
        esink = sb("esink", [128, 32], F32)
        esink_b = Buf("esink")
        WsT = sb("WsT", [128, L, 4, 128], BF16)
        BD = sb("BD", [128, L, 2, 128], BF16)
        wsbd_b = Buf("wsbd")
        cst = sb("cst", [128, NCONST], F32)
        cst_b = Buf("cst")
        identB = sb("identB", [128, 128], BF16)
        mbias = sb("mbias", [128, 2, 512], BF16)
        onesN = sb("onesN", [128, 128], BF16)
        ones1 = sb("ones1", [128, 128], BF16)
        cb_b = Buf("constsB")
        small = sb("small", [128, 96], F32)
        posi = sb("posi", [128, T], I32)
        posi_b = Buf("posi")
        NSLOT = 4
        ring_t = sb("ring", [128, NSLOT, 4096], BF16)
        ring = Rot([(ring_t[:, i, :], Buf("ring%d" % i)) for i in range(NSLOT)])
        ringcnt = [0]

        ps_t = [psum("ps%d" % i, [128, 512], F32) for i in range(8)]
        psr = Rot([(ps_t[i], Buf("ps%d" % i, True)) for i in range(7)])
        ss_ps, ss_b = ps_t[7], Buf("ss", True)

        identF = cst[:, 0:128]
        trilT = cst[:, 128:256]
        mc_f = cst[:, 256:384]
        mp_f = cst[:, 384:512]
        inv_col = cst[:, 512:513]
        sgn_col = cst[:, 513:514]
        invcnt = cst[:, 514:546]
        eps_col = cst[:, 546:547]

        def G(l, j):
            o = (l * 6 + j) * 8
            return pa[:, o:o + 8]

        def gmg(l):
            return pa[:, 192 + 2 * l:192 + 2 * l + 2]

        def psc(l):
            return pa[:, 200 + 2 * l:200 + 2 * l + 2]

        def bfull(l, a):
            o = 240 + (l * 2 + a) * 128
            return pa[:, o:o + 128]

        act_or_dve = Rot(["act", "dve"])

        def multi(eng, fns, reads, writes):
            for f_ in fns:
                S.op(eng, f_, reads, writes)

        def copy_op(eng, out, in_, reads, writes):
            if eng == "act":
                S.op("act", lambda e: e.activation(out=out, in_=in_, func=AF.Copy), reads, writes)
            else:
                S.op(eng, lambda e: e.tensor_copy(out=out, in_=in_), reads, writes)

        split_next = [0]

        def mm_group(out, pairs, reads, writes):
            if split_next[0] > 0 and len(pairs) == 8 and hT_b[0] in reads:
                split_next[0] -= 1
                other = [b_ for b_ in reads if b_ not in hT_b]
                for i, (l_, r_) in enumerate(pairs):
                    S.op("pe", lambda e, l_=l_, r_=r_, i=i: e.matmul(out, lhsT=l_, rhs=r_, start=(i == 0), stop=(i == 7)),
                         other + [hT_b[i]], writes)
                return

            def fn(e):
                n = len(pairs)
                last = None
                for i, (l_, r_) in enumerate(pairs):
                    last = e.matmul(out, lhsT=l_, rhs=r_, start=(i == 0), stop=(i == n - 1))
                return last
            S.op("pe", fn, reads, writes)

        def ring_load(src_ap, nelem, slot_name="ring"):
            ap, b = ring.next()
            k = ringcnt[0] % NSLOT
            ringcnt[0] += 1
            S.op("sp", lambda e: e.dma_start(out=ap[:, 0:nelem], in_=src_ap), reads=[scr_b[src_ap.tensor.name]],
                 writes=[b], dma_slot="ring%d" % k)
            return ap, b

        scr_b = {"was": Buf("was"), "wds": Buf("wds"), "wts": Buf("wts"), "wvs": Buf("wvs")}
        S.op("sp", lambda e: e.dma_start(out=cst[:], in_=cst_d[:, :]), writes=[cst_b], dma_slot="cst")
        S.op("sp", lambda e: e.dma_start(out=pa[:], in_=pa_d[:, :]), writes=[pa_b], dma_slot="pa")
        S.op("sp", lambda e: e.dma_start(out=stage[:, 0:NPB], in_=pb_d[:, :]), writes=st_b, dma_slot="pb")

        conv_ops = {}

        def conv(l, name, lst):
            def fn(e):
                return [e.dma_start(out=o_, in_=i_) for (o_, i_) in lst]
            conv_ops[(l, name)] = S.op("pool", fn, writes=[], dma_slot="cv_%d_%s" % (l, name), ndma=len(lst))

        cv_b = {}

        def emit_conv(l, name):
            def pairs(c0s):
                return [(was[l, c0:c0 + 4].rearrange("c p f -> p c f"), wa_d[l, c0:c0 + 4].rearrange("c p f -> p c f")) for c0 in c0s]
            if name == "win":
                lst = pairs(range(0, 16, 4)) + [(wts[l], wt_d[l])]
            elif name == "mid":
                lst = pairs(list(range(16, 40, 4)) + [84, 88]) + [(wvs[l, h], wv_d[l, h]) for h in range(2)]
            elif name == "gu":
                lst = pairs(range(40, 84, 4))
            else:
                lst = [(wds[l, c], wd_d[l, c]) for c in range(8)]
            conv(l, name, lst)
            b = Buf("cv%d%s" % (l, name))
            b.last_w = conv_ops[(l, name)]
            cv_b[(l, name)] = b
        emit_conv(0, "win")

        def wa_group(c):
            if c < 16:
                return "win"
            if c < 40 or c >= 84:
                return "mid"
            return "gu"

        def load_wa(l, c0, n=4):
            ap, b = ring.next()
            k = ringcnt[0] % NSLOT
            ringcnt[0] += 1
            src = was[l, c0:c0 + n].rearrange("c p f -> p c f")
            dst = ap[:, 0:n * 1024].rearrange("p (c f) -> p c f", c=n)
            S.op("sp", lambda e: e.dma_start(out=dst, in_=src), reads=[cv_b[(l, wa_group(c0))]], writes=[b],
                 dma_slot="ring%d" % k)
            return ap.rearrange("p (c k j) -> p c k j", c=4, k=8), b

        def load_flat(src, nelem, cvkey):
            ap, b = ring.next()
            k = ringcnt[0] % NSLOT
            ringcnt[0] += 1
            S.op("sp", lambda e: e.dma_start(out=ap[:, 0:nelem], in_=src), reads=[cv_b[cvkey]], writes=[b],
                 dma_slot="ring%d" % k)
            return ap, b

        multi("dve", [lambda e: e.memset(onesN[:], 1.0 / 1024.0),
                      lambda e: e.memset(ones1[:], 1.0),
                      lambda e: e.tensor_copy(out=identB[:], in_=identF),
                      lambda e: e.tensor_scalar(out=mbias[:, 0, :].rearrange("p (g q) -> p g q", g=4),
                                                in0=mc_f.rearrange("p (o q) -> p o q", o=1).broadcast_to([128, 4, 128]),
                                                scalar1=-1.0, scalar2=30000.0, op0=ALU.add, op1=ALU.mult),
                      lambda e: e.tensor_scalar(out=mbias[:, 1, :].rearrange("p (g q) -> p g q", g=4),
                                                in0=mp_f.rearrange("p (o q) -> p o q", o=1).broadcast_to([128, 4, 128]),
                                                scalar1=-1.0, scalar2=30000.0, op0=ALU.add, op1=ALU.mult)],
              reads=[cst_b], writes=[cb_b])

        S.op("dve", lambda e: e.memset(Vaug[:], 1.0), writes=[Vaug_b])
        S.op("dve", lambda e: e.memset(Vprev[:], 1.0), writes=Vprev_b)
        S.op("dve", lambda e: e.memset(Kprev[:], 0.0), writes=Kprev_b)
        S.op("dve", lambda e: e.memset(phalo[:], 0.0), writes=phalo_b)
        S.op("dve", lambda e: e.memset(KT2[:], 0.0), writes=[KT2_b])
        S.op("dve", lambda e: e.memset(pT[:], 0.0), writes=[pT_b])
        S.op("dve", lambda e: e.memset(vn_pad[:], 0.0), writes=vn_b)
        S.op("act", lambda e: e.activation(out=esink[:], in_=pa[:, 208:240], func=AF.Exp), reads=[pa_b], writes=[esink_b])
        multi("dve", [lambda e: e.tensor_tensor(out=WsT[:].rearrange("p l g t -> p (l g) t"),
                                                in0=stage[:, 0:4 * L * 128].rearrange("p (m t) -> p m t", t=128),
                                                in1=trilT.rearrange("p (o t) -> p o t", o=1).broadcast_to([128, 4 * L, 128]),
                                                op=ALU.mult),
                      lambda e: e.tensor_copy(out=BD[:].rearrange("p l a d -> p (l a d)"), in_=stage[:, 2048:3072][:, 0:L * 256])],
              reads=st_b + [cst_b], writes=[wsbd_b])

        mem_sb = xT[:].rearrange("p k t -> p (k t)")
        S.op("sp", lambda e: e.dma_start(out=mem_sb[:, 0:2048].rearrange("p (c d) -> p c d", c=2),
                                         in_=mem_d.rearrange("(c p) d -> p c d", p=128)), writes=xT_b, dma_slot="mem")
        memg = stage[:, 3072:4096]

        sm_b = Buf("small_mem")

        multi("act", [lambda e, c=c: e.activation(out=mem_sb[:, 2048 + c * 1024:2048 + (c + 1) * 1024],
                                                  in_=mem_sb[:, c * 1024:(c + 1) * 1024],
                                                  func=AF.Square, accum_out=small[:, c:c + 1]) for c in range(2)],
              reads=xT_b, writes=xT_b + [sm_b])
        S.op("act", lambda e: e.activation(out=small[:, 2:4], in_=small[:, 0:2], func=AF.Sqrt, bias=eps_col, scale=1.0 / D),
             reads=[sm_b, cst_b], writes=[sm_b])
        S.op("dve", lambda e: e.reciprocal(out=small[:, 4:6], in_=small[:, 2:4]), reads=[sm_b], writes=[sm_b])

        multi("dve", [lambda e, c=c: e.scalar_tensor_tensor(out=mem_sb[:, c * 1024:(c + 1) * 1024],
                                                            in0=mem_sb[:, c * 1024:(c + 1) * 1024],
                                                            scalar=small[:, 4 + c:5 + c], in1=memg, op0=ALU.mult, op1=ALU.mult)
                      for c in range(2)], reads=xT_b + st_b + [sm_b], writes=xT_b)
        for kc in range(8):
            ps, pb_ = psr.next()

            def fn(e, ps=ps, kc=kc):
                last = None
                for c in range(2):
                    last = e.transpose(out=ps[:, c * 128:(c + 1) * 128],
                                       in_=mem_sb[:, c * 1024 + kc * 128:c * 1024 + (kc + 1) * 128], identity=identF)
                return last
            S.op("pe", fn, reads=xT_b + [cst_b], writes=[pb_])
            copy_op(act_or_dve.next(), memT[:, kc, :], ps[:, 0:256], [pb_], [memT_b])

        pending_ss = []

        def flush_ss(keep=0):
            while len(pending_ss) > keep:
                sq, sq_b, n = pending_ss.pop(0)
                S.op("pe", lambda e, sq=sq, n=n: e.matmul(ss_ps[:], lhsT=onesN[:], rhs=sq, start=(n == 0), stop=(n == 7)),
                     reads=[sq_b, cb_b], writes=[ss_b])

        dummy_b = Buf("dummy_sqrt")

        def preload_sqrt_table():
            S.op("act", lambda e: e.activation(out=small[:, 90:91], in_=eps_col, func=(AF.Ln if RSTD_LN_EXP else AF.Sqrt)),
                 reads=[cst_b], writes=[dummy_b])

        def rstd_from_ss(as_bf16=False):
            rs, rs_b = ftmp.next()
            if RSTD_LN_EXP:
                S.op("act", lambda e: e.activation(out=rs, in_=ss_ps[:], func=AF.Ln, bias=eps_col, scale=1.0),
                     reads=[ss_b, cst_b], writes=[rs_b])
                if as_bf16:
                    rsb, rsb_b = ptr.next()
                    S.op("act", lambda e: e.activation(out=rsb, in_=rs, func=AF.Exp, scale=-0.5), reads=[rs_b], writes=[rsb_b])
                    return rsb, rsb_b
                S.op("act", lambda e: e.activation(out=rs, in_=rs, func=AF.Exp, scale=-0.5), reads=[rs_b], writes=[rs_b])
            else:
                S.op("act", lambda e: e.activation(out=rs, in_=ss_ps[:], func=AF.Sqrt, bias=eps_col, scale=1.0),
                     reads=[ss_b, cst_b], writes=[rs_b])
                S.op("dve", lambda e: e.reciprocal(out=rs, in_=rs), reads=[rs_b], writes=[rs_b])
            return rs, rs_b

        pend_copy = []

        def pre_stat(kc, nxt):
            sq, sq_b = sqb.next()
            S.op("act", lambda e: e.activation(out=sq, in_=xT[:, kc, :], func=AF.Square), reads=[xT_b[kc]], writes=[sq_b])
            pending_ss.append((sq, sq_b, kc))
            flush_ss(keep=2)
            if kc <= 5:
                flush_copy()
            g = G(*nxt)
            pend_copy.append(lambda: S.op("act", lambda e: e.activation(out=hT[:, kc, :], in_=xT[:, kc, :], func=AF.Copy, scale=g[:, kc:kc + 1]),
                                          reads=[xT_b[kc], pa_b], writes=[hT_b[kc]]))

        def flush_copy():
            while pend_copy:
                pend_copy.pop(0)()

        def pre_scale(l, j):
            flush_ss()
            split_next[0] = 2
            rs, rs_b = rstd_from_ss(as_bf16=PRE_RS_BF16)
            flush_copy()
            for kc in range(8):
                eng = "dve"
                S.op(eng, lambda e, kc=kc: e.tensor_tensor(out=hT[:, kc, :], in0=hT[:, kc, :], in1=rs, op=ALU.mult),
                     reads=[hT_b[kc], rs_b], writes=[hT_b[kc]])

        def prenorm(l, j):
            for kc in range(8):
                pre_stat(kc, (l, j))
            pre_scale(l, j)

        def post_chunk(l, j, n, ps, ps_b_):
            y = stage[:, n * T:(n + 1) * T]
            g = G(l, j)
            sq, sq_b = sqb.next()
            S.op("act", lambda e: e.activation(out=sq, in_=ps[:], func=AF.Square), reads=[ps_b_], writes=[sq_b])
            cp_ = lambda: S.op("act", lambda e: e.activation(out=y, in_=ps[:], func=AF.Copy, scale=g[:, n:n + 1]),
                               reads=[ps_b_, pa_b], writes=[st_b[n]])
            if n == 7:
                pend_copy.append(cp_)
            else:
                cp_()
            pending_ss.append((sq, sq_b, n))
            flush_ss(keep=2)

        def post_finish(nxt):
            flush_ss()
            rs, rs_b = rstd_from_ss()
            flush_copy()

            def mul_op(eng, n):
                y = stage[:, n * T:(n + 1) * T]
                S.op(eng, lambda e: e.tensor_tensor(out=y, in0=y, in1=rs, op=ALU.mult), reads=[st_b[n], rs_b], writes=[st_b[n]])

            def add_op(eng, n):
                y = stage[:, n * T:(n + 1) * T]
                S.op(eng, lambda e: e.tensor_tensor(out=xT[:, n, :], in0=xT[:, n, :], in1=y, op=ALU.add),
                     reads=[st_b[n], xT_b[n]], writes=[xT_b[n]])
                if nxt is not None:
                    pre_stat(n, nxt)
            dve_n = [0, 1, 2, 3, 4, 5, 6, 7]
            pool_n = []
            for lst, eng in ((dve_n, "dve"), (pool_n, "pool")):
                if not lst:
                    continue
                mul_op(eng, lst[0])
                for i_, n in enumerate(lst):
                    if i_ + 1 < len(lst):
                        mul_op(eng, lst[i_ + 1])
                    add_op(eng, n)
            if nxt is not None:
                pre_scale(*nxt)

        def proj_fm(w4, wb, ci, rhs_list, rhs_bufs, kcn=8):
            ps, pb_ = psr.next()
            mm_group(ps[:], [(w4[:, ci, kc, :], rhs_list[kc]) for kc in range(kcn)], reads=[wb] + rhs_bufs, writes=[pb_])
            return ps, pb_

        hT_list = [hT[:, kc, :] for kc in range(8)]

        def chk(stage):
            if stop == stage:
                raise _Stop()

        for ti in range(n_tiles):
            r0 = ti * T
            bp32 = bp_t[:].rearrange("p c t -> p (c t)").bitcast(F32)
            if ti == 0:
                S.op("sp", lambda e, r0=r0: e.dma_start(out=stage[:].rearrange("p (b d) -> p b d", b=4),
                                                        in_=x_d[r0:r0 + T, :].rearrange("(b p) d -> p b d", p=128)),
                     writes=st_b, dma_slot="xin0")
                xin, xin_bufs = stage, st_b
            else:
                xin, xin_bufs = bp32, bp_b[0:16]
            for kc in range(8):
                ps, pb_ = psr.next()

                def fn(e, ps=ps, kc=kc, xin=xin):
                    last = None
                    for b in range(4):
                        last = e.transpose(out=ps[:, b * 128:(b + 1) * 128],
                                           in_=xin[:, b * 1024 + kc * 128:b * 1024 + (kc + 1) * 128], identity=identF)
                    return last
                S.op("pe", fn, reads=list(xin_bufs) + [cst_b], writes=[pb_])
                copy_op("dve", xT[:, kc, :], ps[:], [pb_], [xT_b[kc]])
                sq_, sq_b_ = sqb.next()
                S.op("act", lambda e, sq_=sq_, ps=ps: e.activation(out=sq_, in_=ps[:], func=AF.Square), reads=[pb_], writes=[sq_b_])
                pending_ss.append((sq_, sq_b_, kc))
                flush_ss(keep=2)
                g00 = G(0, 0)
                S.op("act", lambda e, ps=ps, kc=kc, g00=g00: e.activation(out=hT[:, kc, :], in_=ps[:], func=AF.Copy, scale=g00[:, kc:kc + 1]),
                     reads=[pb_, pa_b], writes=[hT_b[kc]])
            S.op("sp", lambda e, r0=r0: e.dma_start(out=posi[:], in_=pos_d[0:1, r0:r0 + T].broadcast_to([128, T])),
                 writes=[posi_b], dma_slot="pos")
            a0, a0_b = ftmp.next()
            a1, a1_b = ftmp.next()
            a2, a2_b = ftmp.next()
            ki = posi

            PI = float(np.pi)
            rb = [a0_b, a1_b, a2_b, posi_b]
            steps = [
                lambda e: e.tensor_copy(out=a0, in_=posi[:]),
                lambda e: e.tensor_scalar(out=a0, in0=a0, scalar1=inv_col, scalar2=None, op0=ALU.mult),
                lambda e: e.tensor_scalar(out=a1, in0=a0, scalar1=1.0 / TWO_PI, scalar2=None, op0=ALU.mult),
                lambda e: e.tensor_copy(out=ki[:], in_=a1),
                lambda e: e.tensor_copy(out=a1, in_=ki[:]),
                lambda e: e.scalar_tensor_tensor(out=a0, in0=a1, scalar=-C1, in1=a0, op0=ALU.mult, op1=ALU.add),
                lambda e: e.scalar_tensor_tensor(out=a0, in0=a1, scalar=-C2, in1=a0, op0=ALU.mult, op1=ALU.add),
                lambda e: e.tensor_scalar(out=a0, in0=a0, scalar1=-PI, scalar2=PI, op0=ALU.max, op1=ALU.min),
                lambda e: e.tensor_scalar(out=a1, in0=a0, scalar1=PI / 2, scalar2=None, op0=ALU.add),
                lambda e: e.tensor_scalar(out=a2, in0=a1, scalar1=PI, scalar2=-TWO_PI, op0=ALU.is_gt, op1=ALU.mult),
                lambda e: e.tensor_tensor(out=a1, in0=a1, in1=a2, op=ALU.add),
                lambda e: e.tensor_scalar(out=a1, in0=a1, scalar1=-PI, scalar2=PI, op0=ALU.max, op1=ALU.min),
            ]
            for fn_ in steps:
                S.op("dve", fn_, reads=rb + [cst_b], writes=rb)

            multi("act", [lambda e: e.activation(out=sinT[:], in_=a0, func=AF.Sin, scale=sgn_col),
                          lambda e: e.activation(out=cosT[:], in_=a1, func=AF.Sin)], reads=[a0_b, a1_b, cst_b], writes=[cs_b])

            try:
                chk('rope')
                for l in range(L):
                    if ti == 0:
                        emit_conv(l, "mid")
                        emit_conv(l, "gu")
                    if l == 0:
                        pre_scale(0, 0)
                    chk('prenorm')
                    S.op("pool", lambda e, l=l: e.tensor_copy(out=KT2[:].rearrange("p a h k -> p (a h) k")[:, :, 0:128], in_=Kprev[:, l, :, :]),
                         reads=[Kprev_b[l]], writes=[KT2_b])
                    S.op("pool", lambda e, l=l: e.tensor_copy(out=Vaug[:, 0, :, :], in_=Vprev[:, l, :, :]),
                         reads=[Vprev_b[l]], writes=[Vaug_b])
                    S.op("pool", lambda e, l=l: e.tensor_copy(out=pT[:, :, 0:16], in_=phalo[:, l, :, :]),
                         reads=[phalo_b[l]], writes=[pT_b])

                    wt_ap, wt_b = load_flat(wts[l], 8 * 384, (l, "win"))
                    wt3 = wt_ap[:, 0:8 * 384].rearrange("p (k n) -> p k n", k=8)
                    vg = stage[:, 0:1024].rearrange("p (b c) -> p b c", b=4)
                    ms = small[:, 8:24]
                    ms_b = Buf("ms")
                    for b in range(4):
                        ps, pb_ = psr.next()
                        mm_group(ps[:, 0:384], [(hT[:, kc, b * 128:(b + 1) * 128], wt3[:, kc, :]) for kc in range(8)],
                                 reads=[wt_b] + hT_b, writes=[pb_])
                        S.op("act", lambda e, ps=ps, b=b: e.activation(out=vg[:, b, :], in_=ps[:, 0:256], func=AF.Gelu_apprx_tanh),
                             reads=[pb_], writes=[st_b[b // 2]])
                        S.op("act", lambda e, ps=ps, b=b: e.activation(out=Vaug[:, 1 + b, :, 0:64],
                                                                        in_=ps[:, 256:384].rearrange("p (h d) -> p h d", h=2), func=AF.Copy),
                             reads=[pb_], writes=[Vaug_b])
                        sq, sq_b = ftmp.next()
                        S.op("dve", lambda e, sq=sq, b=b: e.tensor_tensor(out=sq[:, 0:256], in0=vg[:, b, :], in1=vg[:, b, :], op=ALU.mult),
                             reads=[st_b[b // 2]], writes=[sq_b])
                        S.op("dve", lambda e, sq=sq, b=b: e.tensor_reduce(out=ms[:, 4 * b:4 * b + 4],
                                                                          in_=sq[:, 0:256].rearrange("p (g d) -> p g d", g=4),
                                                                          axis=AX.X, op=ALU.add),
                             reads=[sq_b], writes=[ms_b])

                    w4, wb = load_wa(l, 0)
                    for a in range(2):
                        ps, pb_ = proj_fm(w4, wb, 2 + a, hT_list, hT_b)
                        copy_op("act", pT[:, a, 16:16 + T], ps[:], [pb_], [pT_b])

                    A_ = stage[:, 1024:1024 + 2 * (T + 16)].rearrange("p (a t) -> p a t", a=2)
                    B_ = stage[:, 2560:2560 + 2 * (T + 16)].rearrange("p (a t) -> p a t", a=2)
                    A_bufs = [st_b[2], st_b[3], st_b[4]]
                    B_bufs = [st_b[5], st_b[6], st_b[7]]
                    W_ = T + 16
                    ic = invcnt.rearrange("p (a t) -> p a t", a=2)
                    pchain = A_bufs + B_bufs

                    def pstep(fn_, eng="pool"):
                        S.op(eng, fn_, reads=[pT_b, cst_b] + pchain, writes=pchain)

                    def pfin(src, a, p0, p1, invw, ti=ti):
                        if ti == 0:
                            pstep(lambda e: e.tensor_tensor(out=src[p0:p1, a, 16:32], in0=src[p0:p1, a, 16:32], in1=ic[p0:p1, a, :], op=ALU.mult))
                        S.op("dve", lambda e: e.scalar_tensor_tensor(out=pooled[p0:p1, a, :], in0=src[p0:p1, a, 16:W_], scalar=invw,
                                                                      in1=pT[p0:p1, a, 16:W_], op0=ALU.mult, op1=ALU.subtract),
                             reads=[pT_b] + pchain, writes=[pooled_b])
                    pstep(lambda e: e.tensor_tensor(out=A_[:, :, 1:W_], in0=pT[:, :, 1:W_], in1=pT[:, :, 0:W_ - 1], op=ALU.add))
                    pstep(lambda e: e.tensor_tensor(out=B_[64:128, 0, 3:W_], in0=A_[64:128, 0, 3:W_], in1=A_[64:128, 0, 1:W_ - 2], op=ALU.add))
                    pstep(lambda e: e.tensor_tensor(out=B_[:, 1, 3:W_], in0=A_[:, 1, 3:W_], in1=A_[:, 1, 1:W_ - 2], op=ALU.add))
                    pfin(A_, 0, 0, 64, 0.5)
                    pfin(B_, 0, 64, 128, 0.25)
                    pstep(lambda e: e.tensor_tensor(out=A_[:, 1, 7:W_], in0=B_[:, 1, 7:W_], in1=B_[:, 1, 3:W_ - 4], op=ALU.add))
                    pstep(lambda e: e.tensor_tensor(out=B_[64:128, 1, 15:W_], in0=A_[64:128, 1, 15:W_], in1=A_[64:128, 1, 7:W_ - 8], op=ALU.add))
                    pfin(A_, 1, 0, 64, 0.125)
                    pfin(B_, 1, 64, 128, 0.0625)
                    S.op("pool", lambda e, l=l: e.tensor_copy(out=phalo[:, l, :, :], in_=pT[:, :, T:T + 16]),
                         reads=[pT_b], writes=[phalo_b[l]])

                    for a in range(2):
                        ps, pb_ = proj_fm(w4, wb, a, hT_list, hT_b)
                        S.op("act", lambda e, ps=ps, a=a: e.activation(out=uT[:, a, :], in_=ps[:], func=AF.Gelu_apprx_tanh),
                             reads=[pb_], writes=[uT_b[a]])

                    rv = small[:, 24:40]
                    rv_b = Buf("rv")
                    S.op("act", lambda e: e.activation(out=rv, in_=ms, func=AF.Sqrt, bias=eps_col, scale=1.0 / 64),
                         reads=[ms_b, cst_b], writes=[rv_b])
                    S.op("dve", lambda e: e.reciprocal(out=rv, in_=rv), reads=[rv_b], writes=[rv_b])
                    for b in range(4):
                        base = vn_pad[:, b, 0, :]
                        vo = bass.AP(tensor=base.tensor, offset=base.offset, ap=[list(base.ap[0]), [256, 2], [192, 2], [1, 64]])
                        S.op("dve", lambda e, b=b, vo=vo: e.tensor_tensor(
                            out=vo, in0=vg[:, b, :].rearrange("p (a c d) -> p a c d", a=2, c=2),
                            in1=rv[:, 4 * b:4 * b + 4].rearrange("p (a c o) -> p a c o", a=2, o=1).broadcast_to([128, 2, 2, 64]),
                            op=ALU.mult), reads=[st_b[b // 2], rv_b], writes=[vn_b[b]])

                    def rope_pair(w4, wb, ci, outs, out_b):
                        ps1, pb1 = proj_fm(w4, wb, ci, hT_list, hT_b)
                        ps2, pb2 = proj_fm(w4, wb, ci + 1, hT_list, hT_b)
                        t1, t1_b = ftmp.next()
                        t2, t2_b = ftmp.next()
                        S.op("dve", lambda e: e.tensor_tensor(out=t1, in0=ps1[:], in1=cosT[:], op=ALU.mult),
                             reads=[pb1, cs_b], writes=[t1_b])
                        S.op("dve", lambda e: e.tensor_tensor(out=t2, in0=ps2[:], in1=sinT[:], op=ALU.mult),
                             reads=[pb2, cs_b], writes=[t2_b])
                        for (o_ap, p0, p1) in outs:
                            S.op("pool", lambda e, o_ap=o_ap, p0=p0, p1=p1: e.tensor_tensor(out=o_ap, in0=t1[p0:p1, :], in1=t2[p0:p1, :],
                                                                                         op=ALU.add),
                                 reads=[t1_b, t2_b], writes=[out_b])
                    w4, wb = load_wa(l, 12)
                    for h in range(2):
                        rope_pair(w4, wb, 2 * h, [(KT2[0:64, 0, h, 128:128 + T], 0, 64), (KT2[64:128, 1, h, 128:128 + T], 64, 128)], KT2_b)
                    for grp in range(2):
                        w4, wb = load_wa(l, 4 + 4 * grp)
                        for jj in range(2):
                            j = grp * 2 + jj
                            rope_pair(w4, wb, 2 * jj, [(QT[:, j, :], 0, 128)], QT_b[j])
                    chk('wintm')

                    for a in range(2):
                        ps, pb_ = psr.next()

                        def gm_mm(e, ps=ps, a=a, l=l):
                            last = None
                            for b in range(4):
                                for c in range(2):
                                    g = 2 * a + c
                                    last = e.matmul(ps[:, b * 128:(b + 1) * 128], lhsT=vn_pad[:, b, g, :], rhs=WsT[:, l, g, :],
                                                    start=(c == 0), stop=(c == 1))
                            return last
                        S.op("pe", gm_mm, reads=vn_b + [wsbd_b], writes=[pb_])
                        tmp, tmp_b = ftmp.next()
                        S.op("dve", lambda e, ps=ps, a=a, l=l, tmp=tmp: e.scalar_tensor_tensor(
                            out=tmp.rearrange("p (b t) -> p b t", b=4), in0=ps[:].rearrange("p (b t) -> p b t", b=4),
                            scalar=gmg(l)[:, a:a + 1],
                            in1=bfull(l, a).rearrange("p (o t) -> p o t", o=1).broadcast_to([128, 4, 128]),
                            op0=ALU.mult, op1=ALU.add), reads=[pb_, pa_b], writes=[tmp_b])
                        S.op("pool", lambda e, a=a, tmp=tmp: e.tensor_tensor(out=bp_t[:, a, :], in0=tmp, in1=uT[:, a, :], op=ALU.mult),
                             reads=[tmp_b, uT_b[a]], writes=[bp_b[a]])
                    chk('gm')
                    for a in range(2):
                        ps, pb_ = psr.next()
                        S.op("pe", lambda e, ps=ps, a=a, l=l: e.matmul(ps[:], lhsT=BD[:, l, a, :], rhs=pooled[:, a, :], start=True, stop=True),
                             reads=[pooled_b, wsbd_b], writes=[pb_])
                        S.op("act", lambda e, ps=ps, a=a, l=l: e.activation(out=bp_t[:, 2 + a, :], in_=ps[:], func=AF.Copy,
                                                                            scale=psc(l)[:, a:a + 1]),
                             reads=[pb_, pa_b], writes=[bp_b[2 + a]])
                    chk('pool')

                    its = [(n, h) for n in range(NB) for h in range(2)]
                    cts = {}
                    ctx = {}

                    def swa_A(i):
                        n, h = its[i]
                        has_prev = (ti * NB + n) > 0
                        c = {"n": n, "h": h, "has_prev": has_prev}
                        if h == 0:
                            cts[n] = ctr.next()

                        def s_mm(e, ps, koff, h=h, n=n):
                            e.matmul(ps[:], lhsT=identB[:], rhs=mbias[:, 0 if koff else 1, :], start=True, stop=False)
                            last = None
                            for g in range(4):
                                last = e.matmul(ps[:, g * 128:(g + 1) * 128],
                                                lhsT=KT2[:, g % 2, h, koff + n * 128:koff + (n + 1) * 128],
                                                rhs=QT[:, 2 * h + g // 2, n * 128:(n + 1) * 128], start=False, stop=(g == 3))
                            return last
                        c["psc"], c["psc_b"] = psr.next()
                        S.op("pe", lambda e, ps=c["psc"]: s_mm(e, ps, 128), reads=[KT2_b, cb_b] + QT_b, writes=[c["psc_b"]])
                        if has_prev:
                            c["psp"], c["psp_b"] = psr.next()
                            S.op("pe", lambda e, ps=c["psp"]: s_mm(e, ps, 0), reads=[KT2_b, cb_b] + QT_b, writes=[c["psp_b"]])
                        ctx[i] = c

                    def swa_B(i):
                        c = ctx[i]
                        for key in ("c", "p"):
                            if key == "p" and not c["has_prev"]:
                                continue
                            pt, pt_b = ptr.next()
                            ps, ps_b = c["ps" + key], c["ps" + key + "_b"]
                            c["pt" + key], c["pt" + key + "_b"] = pt, pt_b
                            S.op("act", lambda e, ps=ps, pt=pt: e.activation(out=pt, in_=ps[:], func=AF.Exp, scale=0.125),
                                 reads=[ps_b], writes=[pt_b])

                    def swa_C(i):
                        c = ctx[i]
                        n, h, has_prev = c["n"], c["h"], c["has_prev"]
                        po, po_b = psr.next()
                        pov = po[:, 0:260].rearrange("p (g d) -> p g d", g=4)
                        c["pov"], c["po_b"] = pov, po_b
                        ptc = c["ptc"]
                        ptp = c.get("ptp")

                        def pv_mm(e):
                            last = None
                            for g in range(4):
                                if has_prev:
                                    e.matmul(pov[:, g, :], lhsT=ptp[:, g * 128:(g + 1) * 128], rhs=Vaug[:, n, h, 0:65], start=True, stop=False)
                                last = e.matmul(pov[:, g, :], lhsT=ptc[:, g * 128:(g + 1) * 128], rhs=Vaug[:, n + 1, h, 0:65],
                                                start=(not has_prev), stop=True)
                            return last
                        S.op("pe", pv_mm, reads=[c["ptc_b"], Vaug_b] + ([c["ptp_b"]] if has_prev else []), writes=[po_b])

                    def swa_D(i, l=l):
                        c0, c1 = ctx[i - 1], ctx[i]
                        n = c1["n"]
                        ct, ct_b = cts[n]
                        o_ = 40 + 16 * ((i // 2) % 2)
                        dt_ = small[:, o_:o_ + 8]
                        rc_ = small[:, o_ + 8:o_ + 16]
                        dt_b = Buf("dt")
                        for c in (c0, c1):
                            h = c["h"]
                            S.op("dve", lambda e, c=c, h=h: e.tensor_tensor(
                                out=dt_[:, 4 * h:4 * h + 4].rearrange("p (g o) -> p g o", o=1), in0=c["pov"][:, :, 64:65],
                                in1=esink[:, l * 8 + 4 * h:l * 8 + 4 * h + 4].rearrange("p (g o) -> p g o", o=1), op=ALU.add),
                                reads=[c["po_b"], esink_b], writes=[dt_b])
                        S.op("dve", lambda e: e.reciprocal(out=rc_, in_=dt_), reads=[dt_b], writes=[dt_b])
                        for c in (c0, c1):
                            h = c["h"]
                            S.op("dve", lambda e, c=c, h=h: e.tensor_tensor(
                                out=ct[:, h * 256:(h + 1) * 256].rearrange("p (g d) -> p g d", g=4), in0=c["pov"][:, :, 0:64],
                                in1=rc_[:, 4 * h:4 * h + 4].rearrange("p (g o) -> p g o", o=1).broadcast_to([128, 4, 64]), op=ALU.mult),
                                reads=[c["po_b"], dt_b], writes=[ct_b])

                    def swa_E(n):
                        ct, ct_b = cts[n]
                        pst, pst_b = psr.next()
                        pstb = pst[:].bitcast(BF16)

                        def tr4(e):
                            last = None
                            for j in range(4):
                                last = e.transpose(out=pstb[:, j * 128:(j + 1) * 128], in_=ct[:, j * 128:(j + 1) * 128], identity=identB[:])
                            return last
                        S.op("pe", tr4, reads=[ct_b, cb_b], writes=[pst_b])
                        copy_op("act", bp_t[:, 4:8, n * 128:(n + 1) * 128],
                                pstb[:, 0:512].rearrange("p (j q) -> p j q", j=4), [pst_b], bp_b[4:8])

                    swa_A(0)
                    for i in range(len(its)):
                        if i + 1 < len(its):
                            swa_A(i + 1)
                        swa_B(i)
                        swa_C(i)
                        if i % 2 == 1:
                            swa_D(i)
                        if i % 2 == 0 and i >= 2:
                            swa_E((i - 2) // 2)
                    swa_E(NB - 1)
                    S.op("pool", lambda e, l=l: e.tensor_copy(out=Kprev[:, l, :, :], in_=KT2[:].rearrange("p a h k -> p (a h) k")[:, :, T:T + 128]),
                         reads=[KT2_b], writes=[Kprev_b[l]])
                    S.op("pool", lambda e, l=l: e.tensor_copy(out=Vprev[:, l, :, :], in_=Vaug[:, 4, :, :]),
                         reads=[Vaug_b], writes=[Vprev_b[l]])

                    preload_sqrt_table()
                    chk('swa')
                    mix_list = [bp_t[:, kc, :] for kc in range(8)]
                    for grp in range(2):
                        w4, wb = load_wa(l, OFF_WO + 4 * grp)
                        for ci in range(4):
                            ps, pb_ = proj_fm(w4, wb, ci, mix_list, bp_b[0:8])
                            flush_ss()
                            post_chunk(l, 1, grp * 4 + ci, ps, pb_)
                    post_finish((l, 2))

                    chk('wo')
                    if ti == 0:
                        emit_conv(l, "dn")
                    if ti == 0:
                        mem_list = [memT[:, kc, :] for kc in range(8)]
                        for grp in range(2):
                            w4, wb = load_wa(l, OFF_XK + 4 * grp)
                            for ci in range(4):
                                ps, pb_ = psr.next()
                                mm_group(ps[:, 0:256], [(w4[:, ci, kc, :], mem_list[kc]) for kc in range(8)], reads=[wb, memT_b], writes=[pb_])
                                copy_op(act_or_dve.next(), KmT[:, l, grp * 4 + ci, :], ps[:, 0:256], [pb_], [KmT_b[l]])
                        for half in range(2):
                            wv_ap, wv_b = load_flat(wvs[l, half], 4096, (l, "mid"))
                            wv3 = wv_ap.rearrange("p (k n) -> p k n", k=8)
                            for mc in range(2):
                                ps, pb_ = psr.next()
                                mm_group(ps[:], [(memT[:, kc, mc * 128:(mc + 1) * 128], wv3[:, kc, :]) for kc in range(8)],
                                         reads=[wv_b, memT_b], writes=[pb_])
                                copy_op(act_or_dve.next(), Vm[:, l, mc, half * 512:(half + 1) * 512], ps[:], [pb_], [Vm_b[l]])
                    for grp in range(2):
                        w4, wb = load_wa(l, OFF_XQ + 4 * grp)
                        for ci in range(4):
                            n = grp * 4 + ci
                            ps, pb_ = proj_fm(w4, wb, ci, hT_list, hT_b)
                            copy_op(act_or_dve.next(), bp_t[:, 8 + n, :], ps[:], [pb_], [bp_b[8 + n]])
                    xc = {}

                    def xa_A(hx, l=l):
                        c = {"ps": []}
                        for mc in range(2):
                            ps, pb_ = psr.next()
                            mm_group(ps[:], [(KmT[:, l, 2 * hx + dc, mc * 128:(mc + 1) * 128], bp_t[:, 8 + 2 * hx + dc, :]) for dc in range(2)],
                                     reads=[KmT_b[l], bp_b[8 + 2 * hx], bp_b[9 + 2 * hx]], writes=[pb_])
                            c["ps"].append((ps, pb_))
                        xc[hx] = c

                    def xa_B(hx):
                        c = xc[hx]
                        c["px"] = []
                        for mc in range(2):
                            ps, pb_ = c["ps"][mc]
                            px, px_b = pxr.next()
                            S.op("act", lambda e, ps=ps, px=px: e.activation(out=px, in_=ps[:], func=AF.Exp, scale=1.0 / 16),
                                 reads=[pb_], writes=[px_b])
                            c["px"].append((px, px_b))

                    def xa_C(hx, l=l):
                        c = xc[hx]
                        pxs = c["px"]
                        pd, pd_b = psr.next()
                        mm_group(pd[:], [(ones1[:], pxs[mc][0]) for mc in range(2)], reads=[cb_b, pxs[0][1], pxs[1][1]], writes=[pd_b])
                        c["pd"] = (pd, pd_b)
                        c["po"] = []
                        for dc in range(2):
                            ps, pb_ = psr.next()
                            mm_group(ps[:], [(Vm[:, l, mc, hx * 256 + dc * 128:hx * 256 + (dc + 1) * 128], pxs[mc][0]) for mc in range(2)],
                                     reads=[Vm_b[l], pxs[0][1], pxs[1][1]], writes=[pb_])
                            c["po"].append((ps, pb_))

                    def xa_D(hx):
                        c = xc[hx]
                        pd, pd_b = c["pd"]
                        rc, rc_b = ftmp.next()
                        if RSTD_LN_EXP:
                            S.op("act", lambda e: e.activation(out=rc, in_=pd[:], func=AF.Ln), reads=[pd_b], writes=[rc_b])
                            S.op("act", lambda e: e.activation(out=rc, in_=rc, func=AF.Exp, scale=-1.0), reads=[rc_b], writes=[rc_b])
                        else:
                            S.op("dve", lambda e: e.reciprocal(out=rc, in_=pd[:]), reads=[pd_b], writes=[rc_b])
                        for dc in range(2):
                            ps, pb_ = c["po"][dc]
                            S.op("dve", lambda e, ps=ps, dc=dc: e.tensor_tensor(out=bp_t[:, 16 + 2 * hx + dc, :], in0=ps[:], in1=rc, op=ALU.mult),
                                 reads=[pb_, rc_b], writes=[bp_b[16 + 2 * hx + dc]])
                    xa_A(0)
                    for hx in range(4):
                        xa_B(hx)
                        if hx + 1 < 4:
                            xa_A(hx + 1)
                        xa_C(hx)
                        xa_D(hx)
                    preload_sqrt_table()
                    ox_list = [bp_t[:, 16 + kc, :] for kc in range(8)]
                    for grp in range(2):
                        w4, wb = load_wa(l, OFF_XO + 4 * grp)
                        for ci in range(4):
                            ps, pb_ = proj_fm(w4, wb, ci, ox_list, bp_b[16:24])
                            flush_ss()
                            post_chunk(l, 3, grp * 4 + ci, ps, pb_)
                    post_finish((l, 4))

                    chk('xattn')
                    if ti == 0 and l + 1 < L:
                        emit_conv(l + 1, "win")
                    for grp in range(11):
                        w4, wb = load_wa(l, OFF_GU + 4 * grp)
                        for jj in range(2):
                            j = grp * 2 + jj
                            psg, pbg = proj_fm(w4, wb, 2 * jj, hT_list, hT_b)
                            psu, pbu = proj_fm(w4, wb, 2 * jj + 1, hT_list, hT_b)
                            sg, sg_b = ftmp.next()
                            S.op("act", lambda e, psg=psg, sg=sg: e.activation(out=sg, in_=psg[:], func=AF.Silu), reads=[pbg], writes=[sg_b])
                            S.op("dve", lambda e, psu=psu, sg=sg, j=j: e.tensor_tensor(out=bp_t[:, j, :], in0=psu[:], in1=sg, op=ALU.mult),
                                 reads=[pbu, sg_b], writes=[bp_b[j]])
                    preload_sqrt_table()
                    g_list = [bp_t[:, j, :] for j in range(22)]
                    for n in range(8):
                        wd_ap, wd_b = load_flat(wds[l, n], 2816, (l, "dn"))
                        wd3 = wd_ap[:, 0:2816].rearrange("p (k j) -> p k j", k=22)
                        ps, pb_ = psr.next()
                        mm_group(ps[:], [(wd3[:, j, :], g_list[j]) for j in range(22)], reads=[wd_b] + bp_b[0:22], writes=[pb_])
                        flush_ss()
                        post_chunk(l, 5, n, ps, pb_)
                    if l + 1 == L and ti + 1 < n_tiles:
                        S.op("pool", lambda e, r1=r0 + T: e.dma_start(out=bp32[:, 0:4096].rearrange("p (b d) -> p b d", b=4),
                                                                      in_=x_d[r1:r1 + T, :].rearrange("(b p) d -> p b d", p=128)),
                             writes=bp_b[0:16], dma_slot="xin")
                    post_finish((l + 1, 0) if l + 1 < L else None)
            except _Stop:
                S.op('sp', lambda e: e.dma_start(out=small[:, 56:64], in_=cst_d[:, 539:547]), reads=[], writes=[Buf('dummy')], dma_slot='dummy')

            for b in range(4):
                for jh in range(2):
                    ps, pb_ = psr.next()

                    def fn(e, ps=ps, b=b, jh=jh):
                        last = None
                        for q in range(4):
                            kc = jh * 4 + q
                            last = e.transpose(out=ps[:, q * 128:(q + 1) * 128], in_=xT[:, kc, b * 128:(b + 1) * 128], identity=identF)
                        return last
                    S.op("pe", fn, reads=xT_b[jh * 4:jh * 4 + 4] + [cst_b], writes=[pb_])
                    copy_op(act_or_dve.next(), stage[:, b * 1024 + jh * 512:b * 1024 + (jh + 1) * 512], ps[:], [pb_], [st_b[2 * b + jh]])
            S.op("pool", lambda e, r0=r0: e.dma_start(out=out_d[r0:r0 + T, :].rearrange("(b p) d -> p b d", p=128),
                                                      in_=stage[:].rearrange("p (b d) -> p b d", b=4)),
                 reads=st_b, writes=[], dma_slot="xout")

        nout = n_tiles

        def fin(e):
            for nm_, sem_ in S.slot_sem.items():
                if S.slot_cnt[nm_] > 0:
                    e.wait_ge(sem_, S.slot_cnt[nm_])
            return None
        S.ops["pool"].append(_mk_plain(fin))
        print('sbuf remaining', nc.sbuf_bytes_remaining, 'sems free', nc.free_len())
        S.emit()
    return nc


def _mk_plain(fn):
    o = Op()
    o.eng = "pool"
    o.fn = lambda e: fn(e)
    o.is_dma = False
    o.sig = False
    o.waits = []
    o.deps = []
    o.ndma = 0
    o.sem = None
    o.val = None
    return o


def _consts():
    c = np.zeros((128, NCONST), np.float32)
    i = np.arange(128)
    c[:, 0:128] = np.eye(128, dtype=np.float32)
    c[:, 128:256] = (i[:, None] <= i[None, :])
    c[:, 256:384] = (i[:, None] <= i[None, :])
    c[:, 384:512] = (i[:, None] > i[None, :])
    half = 32
    inv = (np.float32(10000.0) ** (-np.arange(half, dtype=np.float32) / np.float32(half))).astype(np.float32)
    c[:, 512] = inv[i % 32]
    c[:, 513] = np.where((i % 64) < 32, -1.0, 1.0)
    wins = (2, 4, 8, 16)
    t = np.arange(16)
    for a in range(2):
        for p in range(128):
            w = wins[2 * a + p // 64]
            c[p, 514 + a * 16:514 + (a + 1) * 16] = float(w) / np.minimum(t + 1, w)
    c[:, 546] = EPS
    return c


def _chunks(w, cols):
    K = w.shape[0]
    kc = K // 128
    ws = w[:, cols]
    n = ws.shape[1] // 128
    return np.ascontiguousarray(ws.reshape(kc, 128, n, 128).transpose(2, 1, 0, 3)).reshape(n, 128, kc * 128)


def prep_shared(inp, depth=DEPTH):
    L = depth
    f = np.float32
    def qcols(j, rot):
        cols = []
        for h in (2 * j, 2 * j + 1):
            base = 768 + h * 64
            if rot:
                cols += list(range(base + 32, base + 64)) + list(range(base, base + 32))
            else:
                cols += list(range(base, base + 64))
        return cols

    def kcols(h, rot):
        base = 1280 + h * 64
        one = (list(range(base + 32, base + 64)) + list(range(base, base + 32))) if rot else list(range(base, base + 64))
        return one + one
    fm = list(range(0, 256)) + list(range(512, 768))
    for j in range(4):
        fm += qcols(j, False) + qcols(j, True)
    for h in range(2):
        fm += kcols(h, False) + kcols(h, True)
    fm = np.array(fm)
    tm = np.array(list(range(256, 512)) + list(range(1408, 1536)))
    gu = []
    for j in range(22):
        gu += list(range(j * 128, (j + 1) * 128)) + list(range(DFF + j * 128, DFF + (j + 1) * 128))
    gu = np.array(gu)
    allc = np.arange(1024)
    wa = np.empty((L, NWA, 128, 1024), f)
    wd = np.empty((L, 8, 128, 2816), f)
    wt = np.empty((L, 128, 8 * 384), f)
    wv = np.empty((L, 2, 128, 4096), f)
    for l in range(L):
        wa[l, 0:16] = _chunks(inp["w_in"][l], fm)
        wa[l, 16:24] = _chunks(inp["w_o"][l], allc)
        wa[l, 24:32] = _chunks(inp["w_xq"][l], allc)
        wa[l, 32:40] = _chunks(inp["w_xo"][l], allc)
        wa[l, 40:84] = _chunks(inp["w_gate_up"][l], gu)
        wa[l, 84:92] = _chunks(inp["w_xkv"][l], allc)
        wd[l] = _chunks(inp["w_down"][l], allc)
        wt[l] = inp["w_in"][l][:, tm].reshape(8, 128, 384).transpose(1, 0, 2).reshape(128, 8 * 384)
        for h in range(2):
            wv[l, h] = inp["w_xkv"][l][:, 1024 + h * 512:1024 + (h + 1) * 512].reshape(8, 128, 512).transpose(1, 0, 2).reshape(128, 4096)
    pa = np.zeros((128, NPA), f)
    gs = [inp[k] for k in ("mix_pre_g", "mix_post_g", "x_pre_g", "x_post_g", "ffn_pre_g", "ffn_post_g")]
    for l in range(L):
        for j in range(6):
            o = (l * 6 + j) * 8
            pa[:, o:o + 8] = gs[j][l].reshape(8, 128).T
        pa[:, 192 + 2 * l:194 + 2 * l] = inp["gm_v_g"][l].reshape(2, 128).T
        pa[:, 200 + 2 * l:202 + 2 * l] = inp["pool_scale"][l].reshape(2, 128).T
        pa[:, 208 + 8 * l:216 + 8 * l] = inp["attn_sinks"][l][None, :]
        bf = np.repeat(inp["gm_b_s"][l], 64, axis=0).reshape(2, 128, 128)
        for a in range(2):
            o = 240 + (l * 2 + a) * 128
            pa[:, o:o + 128] = bf[a]
    pb = np.zeros((128, NPB), f)
    for l in range(L):
        for g in range(4):
            o = (l * 4 + g) * 128
            pb[:, o:o + 128] = inp["gm_w_s"][l, g].T
        for a in range(2):
            o = 2048 + (l * 2 + a) * 128
            pb[0:64, o:o + 64] = inp["pool_w"][l, 2 * a]
            pb[64:128, o + 64:o + 128] = inp["pool_w"][l, 2 * a + 1]
    pb[:, 3072:4096] = inp["mem_norm_g"][None, :]
    return {"cst": _consts(), "pa": pa, "pb": pb, "wa": wa, "wd": wd, "wt": wt, "wv": wv}


_CACHE = {}


def kernel(**inputs):
    inp = {k: np.asarray(v) for k, v in inputs.items()}
    B = inp["x"].shape[0]
    shared = prep_shared(inp)
    if "nc" not in _CACHE:
        _CACHE["nc"] = build_program()
    nc = _CACHE["nc"]
    in_maps = []
    for b in range(B):
        m = dict(shared)
        m["x"] = np.ascontiguousarray(inp["x"][b])
        m["mem"] = np.ascontiguousarray(inp["mem"][b])
        m["pos"] = np.ascontiguousarray(inp["positions"][b].astype(np.int32)).reshape(1, -1)
        in_maps.append(m)
    res = run_bass_kernel_spmd(nc, in_maps, core_ids=list(range(B)))
    return np.stack([r["out"] for r in res.results], axis=0).astype(np.float32)
```
